# Optimizing a Trainium2 kernel written in Bass

```python
import jax, jax.numpy as jnp
from jax import lax
import numpy as np

D_MODEL = 1024
BATCH = 4
SEQ = 8192
DEPTH = 2

N_MIXERS = 2
ATTN_GROUPS = ((128, 1), (512, 4), (2048, 16))
N_ATTN_GROUPS = len(ATTN_GROUPS)
HEADS_PER_GROUP = 16
HEAD_DIM = D_MODEL // HEADS_PER_GROUP
ATTN_WIDTH = HEADS_PER_GROUP * HEAD_DIM
ATTN_QKV_COLS = N_ATTN_GROUPS * 3 * ATTN_WIDTH
SGU_CHUNK = 128
SGU_WIDTH = 2 * D_MODEL
SGU_GROUP_CH = 128
SGU_GROUPS = SGU_WIDTH // SGU_GROUP_CH
DEEPNORM_ALPHA = (2 * DEPTH) ** 0.25
DEEPNORM_BETA = (8 * DEPTH) ** -0.25
LN_EPS = 1e-5
NEG_INF = -1e30
N_A_LAYERS = (DEPTH + 1) // 2
N_B_LAYERS = DEPTH // 2

kernel_name = "hybrid_dilated_swa_sgu_encoder"


def layer_norm(x, g, b):
    xf = x.astype(jnp.float32)
    mu = jnp.mean(xf, axis=-1, keepdims=True)
    var = jnp.mean(jnp.square(xf - mu), axis=-1, keepdims=True)
    return ((xf - mu) * lax.rsqrt(var + LN_EPS) * g.astype(jnp.float32)
            + b.astype(jnp.float32)).astype(x.dtype)


def alibi_slopes(n):
    return 2.0 ** (-8.0 * jnp.arange(1, n + 1, dtype=jnp.float32) / n)


def dilated_window_attention(q, k, v, slopes, window, dilation):
    B, S, H, Dh = q.shape
    d = dilation
    L = S // d
    w = window // (2 * d)
    nb = -(-L // w)
    Lp = nb * w

    def strided(a):
        return a.reshape(B, L, d, H, Dh).transpose(0, 2, 3, 1, 4)

    qb = jnp.pad(strided(q), ((0, 0),) * 3 + ((0, Lp - L), (0, 0))).reshape(B, d, H, nb, w, Dh)
    pad_kv = ((0, 0),) * 3 + ((w, Lp - L + w), (0, 0))

    def band(a):
        ab = jnp.pad(strided(a), pad_kv).reshape(B, d, H, nb + 2, w, Dh)
        return jnp.concatenate([ab[:, :, :, :-2], ab[:, :, :, 1:-1], ab[:, :, :, 2:]], axis=4)

    kb, vb = band(k), band(v)
    rel = (jnp.arange(3 * w)[None, :] - w) - jnp.arange(w)[:, None]
    key_idx = jnp.arange(nb)[:, None] * w - w + jnp.arange(3 * w)[None, :]
    valid = (jnp.abs(rel) <= w)[None] & ((key_idx >= 0) & (key_idx < L))[:, None, :]
    dist = (d * jnp.abs(rel)).astype(jnp.float32)

    s = jnp.einsum('bzhnqd,bzhnkd->bzhnqk', qb, kb).astype(jnp.float32) * (Dh ** -0.5)
    s = s - slopes[:, None, None, None] * dist
    s = jnp.where(valid, s, NEG_INF)
    m = jnp.max(s, axis=-1, keepdims=True)
    p = jnp.exp(s - m)
    den = jnp.sum(p, axis=-1, keepdims=True)
    o = jnp.einsum('bzhnqk,bzhnkd->bzhnqd', p, vb.astype(jnp.float32)) / den
    lse = (m + jnp.log(den))[..., 0]

    o = o.reshape(B, d, H, Lp, Dh)[:, :, :, :L].transpose(0, 3, 1, 2, 4).reshape(B, S, H, Dh)
    lse = lse.reshape(B, d, H, Lp)[:, :, :, :L].transpose(0, 3, 1, 2).reshape(B, S, H)
    return o, lse


def mixer_dilated_attention(h, w_in, w_out, slopes):
    B, S, _ = h.shape
    z = h @ w_in
    qkv = z[..., :ATTN_QKV_COLS].reshape(B, S, N_ATTN_GROUPS, 3, HEADS_PER_GROUP, HEAD_DIM)
    gate = z[..., ATTN_QKV_COLS:]
    outs, lses = [], []
    for g, (window, dil) in enumerate(ATTN_GROUPS):
        o, l = dilated_window_attention(qkv[:, :, g, 0], qkv[:, :, g, 1], qkv[:, :, g, 2],
                                        slopes[g], window, dil)
        outs.append(o)
        lses.append(l)
    wts = jax.nn.softmax(jnp.stack(lses, axis=0), axis=0)
    o = jnp.sum(wts[..., None] * jnp.stack(outs, axis=0), axis=0)
    y = o.reshape(B, S, ATTN_WIDTH).astype(h.dtype) * jax.nn.silu(gate)
    return y @ w_out


def mixer_spatial_gating(h, w_in, ln_g, ln_b, w_s, b_s, w_out):
    B, S, _ = h.shape
    E = SGU_WIDTH
    z = h @ w_in
    uv = jax.nn.gelu(z[..., :2 * E], approximate=False)
    gate = z[..., 2 * E:]
    u, v = uv[..., :E], uv[..., E:]
    v = layer_norm(v, ln_g, ln_b)
    vc = v.reshape(B, S // SGU_CHUNK, SGU_CHUNK, SGU_GROUPS, SGU_GROUP_CH)
    sv = jnp.einsum('gts,bnsgc->bntgc', w_s, vc) + b_s.T[None, None, :, :, None]
    y = u * sv.reshape(B, S, E) * jax.nn.silu(gate)
    return y @ w_out


def setup_inputs(seed: int = 0) -> dict:
    key = jax.random.key(seed)
    ks = jax.random.split(key, 16)
    D, E, AW = D_MODEL, SGU_WIDTH, ATTN_WIDTH
    nrm = jax.random.normal
    f32 = jnp.float32
    return {
        "x": nrm(ks[0], (BATCH, SEQ, D), f32),
        "c": nrm(ks[1], (BATCH, D), f32),
        "ada_w": nrm(ks[2], (DEPTH, D, 3 * D), f32) * (0.5 * D ** -0.5),
        "ada_b": nrm(ks[3], (DEPTH, 3 * D), f32) * 0.02,
        "post_ln_g": 1.0 + 0.05 * nrm(ks[4], (DEPTH, D), f32),
        "post_ln_b": 0.02 * nrm(ks[5], (DEPTH, D), f32),
        "a_w_in": nrm(ks[6], (N_A_LAYERS, D, ATTN_QKV_COLS + AW), f32) * D ** -0.5,
        "a_w_out": nrm(ks[7], (N_A_LAYERS, AW, D), f32) * (AW ** -0.5 * DEEPNORM_BETA),
        "b_w_in": nrm(ks[8], (N_B_LAYERS, D, 3 * E), f32) * D ** -0.5,
        "b_ln_g": 1.0 + 0.05 * nrm(ks[9], (N_B_LAYERS, E), f32),
        "b_ln_b": 0.02 * nrm(ks[10], (N_B_LAYERS, E), f32),
        "b_w_s": nrm(ks[11], (N_B_LAYERS, SGU_GROUPS, SGU_CHUNK, SGU_CHUNK), f32) * SGU_CHUNK ** -0.5,
        "b_b_s": 1.0 + 0.1 * nrm(ks[12], (N_B_LAYERS, SGU_GROUPS, SGU_CHUNK), f32),
        "b_w_out": nrm(ks[13], (N_B_LAYERS, E, D), f32) * (E ** -0.5 * DEEPNORM_BETA),
    }


def reference(x, c, ada_w, ada_b, post_ln_g, post_ln_b, a_w_in, a_w_out,
              b_w_in, b_ln_g, b_ln_b, b_w_s, b_b_s, b_w_out):
    slopes = alibi_slopes(N_ATTN_GROUPS * HEADS_PER_GROUP).reshape(N_ATTN_GROUPS, HEADS_PER_GROUP)
    cond = jax.nn.silu(c)
    for i in range(DEPTH):
        mod = cond @ ada_w[i] + ada_b[i]
        shift, scale, gate = jnp.split(mod, 3, axis=-1)
        h = x * (1.0 + scale[:, None, :]) + shift[:, None, :]
        j = i // N_MIXERS
        if i % N_MIXERS == 0:
            y = mixer_dilated_attention(h, a_w_in[j], a_w_out[j], slopes)
        else:
            y = mixer_spatial_gating(h, b_w_in[j], b_ln_g[j], b_ln_b[j], b_w_s[j], b_b_s[j], b_w_out[j])
        x = layer_norm(DEEPNORM_ALPHA * x + gate[:, None, :] * y, post_ln_g[i], post_ln_b[i])
    return x
```

```python
import contextlib
import numpy as np
import concourse.bass as bass
import concourse.mybir as mybir
from concourse.bass_utils import run_bass_kernel_spmd

F32 = mybir.dt.float32
BF16 = mybir.dt.bfloat16
AF = mybir.ActivationFunctionType
ALU = mybir.AluOpType

D = 1024
SEQ = 8192
TOWN = 4096
TLOC = 5120
NEGBIG = -1.0e6
ALPHA = float((2 * 2) ** 0.25)
EPS = 1e-5
GROUP_D = (1, 4, 16)
SLOPES = [[2.0 ** (-8.0 * (g * 16 + h + 1) / 48.0) for h in range(16)] for g in range(3)]


class Sched:
    def __init__(self, nc, es):
        self.nc = nc
        self.es = es
        self.eng = {"pe": nc.tensor, "act": nc.scalar, "dve": nc.vector, "pool": nc.gpsimd, "sp": nc.sync}
        self.sems = {}
        self.count = {}
        for k in ("pe", "act", "dve", "pool"):
            self.sems[k] = es.enter_context(nc.semaphore("sem_" + k))
            self.count[k] = 0
        self.waited = {k: {} for k in self.eng}
        self.last_w = {}
        self.readers = {}

    def _dsem(self, key):
        if key not in self.sems:
            self.sems[key] = self.es.enter_context(self.nc.semaphore("dsem_" + key))
            self.count[key] = 0
        return self.sems[key]

    def _wait(self, eng, ev):
        key, val = ev
        if self.waited[eng].get(key, 0) >= val:
            return
        self.eng[eng].wait_ge(self.sems[key], val)
        self.waited[eng][key] = val

    def op(self, eng, fn, reads=(), writes=(), dma=None):
        deps = set()
        for r in reads:
            ev = self.last_w.get(r)
            if ev is not None:
                deps.add(ev)
        for w in writes:
            ev = self.last_w.get(w)
            if ev is not None:
                deps.add(ev)
            for ev in self.readers.get(w, ()):
                deps.add(ev)
        for ev in deps:
            if eng == "pe" and ev[0] == "pe":
                continue
            self._wait(eng, ev)
        res = fn()
        if dma is not None:
            sem = self._dsem(dma)
            insts = res if isinstance(res, (list, tuple)) else [res]
            for ins in insts:
                ins.then_inc(sem, 16)
            self.count[dma] += 16 * len(insts)
            ev = (dma, self.count[dma])
        else:
            res.then_inc(self.sems[eng], 1)
            self.count[eng] += 1
            ev = (eng, self.count[eng])
        for w in writes:
            self.last_w[w] = ev
            self.readers[w] = []
        for r in reads:
            self.readers.setdefault(r, []).append(ev)
        return ev

    def barrier(self):
        for e in self.eng:
            for key, cnt in self.count.items():
                if cnt > 0:
                    self._wait(e, (key, cnt))
        self.last_w = {}
        self.readers = {}


def build(stop_after_l0=False):
    nc = bass.Bass("TRN2", target_bir_lowering=False)

    def din(name, shape, dt=F32):
        return nc.dram_tensor(name, shape, dt, kind="ExternalInput").ap()

    x_d = din("x", [TLOC, D])
    ct_d = din("ct", [128, 8])
    adaw_d = din("ada_w", [2, D, 3 * D])
    adab_d = din("ada_b", [2, 3 * D])
    plg_d = din("post_ln_g", [2, D])
    plb_d = din("post_ln_b", [2, D])
    awin_d = din("a_w_in", [D, 10240])
    awout_d = din("a_w_out", [D, D])
    bwin_d = din("b_w_in", [D, 6144])
    blg_d = din("b_ln_g", [1, 2048])
    blb_d = din("b_ln_b", [1, 2048])
    wsT_d = din("w_sT", [128, 16, 128])
    wsN_d = din("w_sN", [128, 16, 128])
    bsT_d = din("b_sT", [128, 16])
    bwout_d = din("b_w_out", [2048, D])
    yts_d = nc.dram_tensor("yts", [128, 8, TOWN], BF16, kind="Internal").ap()
    if stop_after_l0:
        x1s_d = nc.dram_tensor("x1s", [TOWN, D], F32, kind="ExternalOutput").ap()
        out_d = None
    else:
        x1s_d = nc.dram_tensor("x1s", [TOWN, D], F32, kind="Internal").ap()
        out_d = nc.dram_tensor("out", [TOWN, D], F32, kind="ExternalOutput").ap()

    with contextlib.ExitStack() as es:
        S = Sched(nc, es)
        op = S.op
        pe, act, dve, pool, sp = nc.tensor, nc.scalar, nc.vector, nc.gpsimd, nc.sync

        def sbuf(stack, name, shape, dt):
            return stack.enter_context(nc.sbuf_tensor(name, shape, dt))

        ps = es.enter_context(nc.psum_tensor("ps", [128, 8, 512], F32))
        psflat = ps[:].rearrange("p b n -> p (b n)")

        def bank(b):
            return ps[:, b, :]

        def B(*bs):
            return ["ps%d" % b for b in bs]

        ident_f = sbuf(es, "ident_f", [128, 128], F32)
        ident_b = sbuf(es, "ident_b", [128, 128], BF16)
        ones_bf = sbuf(es, "ones_bf", [128, 64], BF16)
        ones_row = sbuf(es, "ones_row", [1, 128], F32)
        cond = sbuf(es, "cond", [128, 8], F32)
        modT = [sbuf(es, "modT%d" % i, [128, 24], F32) for i in range(2)]
        PH = {}
        stat = sbuf(es, "stat", [128, 4, 6], F32)
        mv = sbuf(es, "mv", [128, 2], F32)
        rstd = sbuf(es, "rstd", [128, 1], F32)
        nmr = sbuf(es, "nmr", [128, 1], F32)
        eps_t = sbuf(es, "eps_t", [128, 1], F32)

        def mk_ident():
            pool.memset(ident_f[:], 0.0)
            return pool.affine_select(out=ident_f[:], in_=ident_f[:], pattern=[[-1, 128]],
                                      compare_op=ALU.not_equal, fill=1.0, base=0, channel_multiplier=1)
        op("pool", mk_ident, writes=["ident_f"])
        op("dve", lambda: dve.tensor_copy(out=ident_b[:], in_=ident_f[:]), reads=["ident_f"], writes=["ident_b"])
        op("pool", lambda: pool.memset(ones_bf[:], 1.0), writes=["ones_bf"])
        op("pool", lambda: pool.memset(ones_row[:], 1.0), writes=["ones_row"])
        op("pool", lambda: pool.memset(eps_t[:], EPS), writes=["eps_t"])

        ct_sb = sbuf(es, "ct_sb", [128, 8], F32)
        op("sp", lambda: sp.dma_start(out=ct_sb[:], in_=ct_d[:, :]), writes=["ct_sb"], dma="ct")
        op("act", lambda: act.activation(out=cond[:], in_=ct_sb[:], func=AF.Silu), reads=["ct_sb"], writes=["cond"])

        def phase_params(i, stack):
            PH["gate_bc"] = sbuf(stack, "gate_bc%d" % i, [128, D], F32)
            PH["plg_bc"] = sbuf(stack, "plg_bc%d" % i, [128, D], F32)
            PH["plb_bc"] = sbuf(stack, "plb_bc%d" % i, [128, D], F32)
            gate_bc, plg_bc, plb_bc = PH["gate_bc"], PH["plg_bc"], PH["plb_bc"]
            op("sp", lambda: sp.dma_start(out=plg_bc[:], in_=plg_d[i:i + 1, :].partition_broadcast(128)),
               writes=["plg_bc"], dma="plg")
            op("sp", lambda: sp.dma_start(out=plb_bc[:], in_=plb_d[i:i + 1, :].partition_broadcast(128)),
               writes=["plb_bc"], dma="plb")
            with contextlib.ExitStack() as tmp:
                ones_f = sbuf(tmp, "ones_f%d" % i, [128, 128], F32)
                dg = sbuf(tmp, "dg%d" % i, [128, 8, 128], F32)
                op("pool", lambda: pool.memset(ones_f[:], 1.0), writes=["ones_f"])

                def mkd():
                    last = None
                    for j in range(8):
                        last = dve.tensor_scalar(out=dg[:, j, :], in0=ident_f[:], scalar1=modT[i][:, 16 + j:17 + j],
                                                 scalar2=None, op0=ALU.mult)
                    return last
                op("dve", mkd, reads=["ident_f", "modT%d" % i], writes=["dg"])

                def gbc():
                    last = None
                    for j in range(8):
                        last = pe.matmul(ps[:, 4 + j // 4, (j % 4) * 128:(j % 4 + 1) * 128], lhsT=ones_f[:], rhs=dg[:, j, :],
                                         start=True, stop=True)
                    return last
                op("pe", gbc, reads=["ones_f", "dg"], writes=B(4, 5))
                op("dve", lambda: dve.tensor_copy(out=gate_bc[:], in_=psflat[:, 4 * 512:6 * 512]),
                   reads=B(4, 5), writes=["gate_bc"])
                S.barrier()

        def mod_steps(i, stack, nslots, banks=(0, 1, 2)):
            stg = [sbuf(stack, "adastg%d_%d" % (i, s), [128, 8, 128], F32) for s in range(nslots)]
            adab_sb = [sbuf(stack, "adab_sb%d_%d" % (i, s), [1, 128], F32) for s in range(nslots)]
            modrow = [sbuf(stack, "modrow%d_%d" % (i, s), [1, 128], F32) for s in range(2)]
            adav = adaw_d[i].rearrange("(kc p) n -> p kc n", p=128)

            def load(j):
                s = j % nslots
                op("sp", lambda: sp.dma_start(out=stg[s][:], in_=adav[:, :, j * 128:(j + 1) * 128]),
                   writes=["adastg%d" % s], dma="adastg%d" % s)
                op("sp", lambda: sp.dma_start(out=adab_sb[s][:], in_=adab_d[i:i + 1, j * 128:(j + 1) * 128]),
                   writes=["adab_sb%d" % s], dma="adab%d" % s)

            def step(j):
                if j == 0:
                    for jj in range(min(nslots, 24)):
                        load(jj)
                s = j % nslots
                bk = banks[j % 2]
                ms = j % 2
                bc = banks[2]

                def mm():
                    last = None
                    for kc in range(8):
                        last = pe.matmul(ps[0:1, bk, 0:128], lhsT=cond[:, kc:kc + 1], rhs=stg[s][:, kc, :],
                                         start=(kc == 0), stop=(kc == 7))
                    return last
                op("pe", mm, reads=["cond", "adastg%d" % s], writes=B(bk))
                plus = 1.0 if 8 <= j < 16 else 0.0
                op("dve", lambda: dve.scalar_tensor_tensor(out=modrow[ms][:], in0=ps[0:1, bk, 0:128], scalar=plus,
                                                           in1=adab_sb[s][:], op0=ALU.add, op1=ALU.add),
                   reads=B(bk) + ["adab_sb%d" % s], writes=["modrow%d" % ms])
                op("pe", lambda: pe.matmul(ps[:, bc, 0:1], lhsT=modrow[ms][0:1, :], rhs=ones_row[0:1, 0:1],
                                           start=True, stop=True),
                   reads=["modrow%d" % ms, "ones_row"], writes=B(bc))
                op("dve", lambda: dve.tensor_copy(out=modT[i][:, j:j + 1], in_=ps[:, bc, 0:1]), reads=B(bc),
                   writes=["modT%d_%d" % (i, j)])
                if j + nslots < 24:
                    load(j + nslots)

            def fin():
                pass
            return [(lambda j=j: step(j)) for j in range(24)], fin

        ST0 = {"stat": stat, "mv": mv, "rstd": rstd, "nmr": nmr, "sfx": ""}

        def epi_E1(yo_b0, xres_ap, xres_name, o_tile, o_name, T):
            sfx = T["sfx"]
            yo = psflat[:, yo_b0 * 512:(yo_b0 + 2) * 512]
            op("dve", lambda: dve.scalar_tensor_tensor(out=o_tile[:], in0=xres_ap, scalar=ALPHA, in1=yo,
                                                       op0=ALU.mult, op1=ALU.add),
               reads=B(yo_b0, yo_b0 + 1) + [xres_name], writes=[o_name])

            def st():
                dve.bn_stats(out=T["stat"][:, 0, :], in_=o_tile[:, 0:512])
                return dve.bn_stats(out=T["stat"][:, 1, :], in_=o_tile[:, 512:1024])
            op("dve", st, reads=[o_name], writes=["stat0" + sfx, "stat1" + sfx])
            op("dve", lambda: dve.bn_aggr(out=T["mv"][:], in_=T["stat"][:, 0:2, :].rearrange("p a b -> p (a b)")),
               reads=["stat0" + sfx, "stat1" + sfx], writes=["mv" + sfx])
            op("act", lambda: act.activation(out=T["rstd"][:], in_=T["mv"][:, 1:2], func=AF.Sqrt, bias=eps_t[:], scale=1.0),
               reads=["mv" + sfx, "eps_t"], writes=["rstd" + sfx])

        def epi_E2(o_tile, o_name, dst_ap, dst_key, T, gb="pool"):
            sfx = T["sfx"]
            op("dve", lambda: dve.reciprocal(out=T["rstd"][:], in_=T["rstd"][:]), reads=["rstd" + sfx], writes=["rstd" + sfx])
            op("dve", lambda: dve.scalar_tensor_tensor(out=T["nmr"][:], in0=T["mv"][:, 0:1], scalar=-1.0, in1=T["rstd"][:],
                                                       op0=ALU.mult, op1=ALU.mult),
               reads=["mv" + sfx, "rstd" + sfx], writes=["nmr" + sfx])
            op("act", lambda: act.activation(out=o_tile[:], in_=o_tile[:], func=AF.Identity, bias=T["nmr"][:],
                                             scale=T["rstd"][:]),
               reads=[o_name, "rstd" + sfx, "nmr" + sfx], writes=[o_name])
            g_eng = "pool" if gb == "pool" else "dve"
            ge = pool if g_eng == "pool" else dve
            op(g_eng, lambda: ge.tensor_tensor(out=o_tile[:], in0=o_tile[:], in1=PH["plg_bc"][:], op=ALU.mult),
               reads=[o_name, "plg_bc"], writes=[o_name])
            op("pool", lambda: pool.tensor_tensor(out=o_tile[:], in0=o_tile[:], in1=PH["plb_bc"][:], op=ALU.add),
               reads=[o_name, "plb_bc"], writes=[o_name])
            op("pool", lambda: pool.dma_start(out=dst_ap, in_=o_tile[:]), reads=[o_name], dma=dst_key)

        def epilogue(yo_b0, xres_ap, xres_name, o_tile, o_name, dst_ap, dst_key, gb="pool"):
            epi_E1(yo_b0, xres_ap, xres_name, o_tile, o_name, ST0)
            epi_E2(o_tile, o_name, dst_ap, dst_key, ST0, gb)

        with contextlib.ExitStack() as l0:
            hT = sbuf(l0, "hT", [128, 8, 16, TLOC // 16], BF16)
            base_reg = sbuf(l0, "base_reg", [128, 512], F32)
            base_first = sbuf(l0, "base_first", [128, 512], F32)

            with contextlib.ExitStack() as st0:
                tmpb = sbuf(st0, "tmpb", [128, 512], F32)
                op("pool", lambda: pool.iota(base_reg[:], pattern=[[0, 2], [128, 2], [-1, 128]], base=-64,
                                             channel_multiplier=1, allow_small_or_imprecise_dtypes=True),
                   writes=["base_reg"])
                op("dve", lambda: dve.scalar_tensor_tensor(out=base_reg[:], in0=base_reg[:], scalar=-1.0, in1=base_reg[:],
                                                           op0=ALU.mult, op1=ALU.min),
                   reads=["base_reg"], writes=["base_reg"])
                op("dve", lambda: dve.tensor_scalar(out=tmpb[:], in0=base_reg[:], scalar1=-64.0, scalar2=-NEGBIG,
                                                    op0=ALU.is_lt, op1=ALU.mult), reads=["base_reg"], writes=["tmpb"])
                op("dve", lambda: dve.tensor_tensor(out=base_reg[:], in0=base_reg[:], in1=tmpb[:], op=ALU.subtract),
                   reads=["base_reg", "tmpb"], writes=["base_reg"])
                op("dve", lambda: dve.tensor_copy(out=base_first[:], in_=base_reg[:]), reads=["base_reg"],
                   writes=["base_first"])
                op("dve", lambda: dve.memset(base_first[0:64, 0:128], NEGBIG), reads=["base_first"], writes=["base_first"])

                with contextlib.ExitStack() as cm0:
                    steps0, fin0 = mod_steps(0, cm0, 6)
                    for st in steps0:
                        st()
                    fin0()
                    S.barrier()

                xs = [sbuf(st0, "xs%d" % s, [128, D], F32) for s in range(3)]
                for tt in range(TLOC // 128):
                    s = tt % 3
                    op("sp", lambda: sp.dma_start(out=xs[s][:], in_=x_d[tt * 128:(tt + 1) * 128, :]),
                       writes=["xs%d" % s], dma="xs%d" % s)
                    b0 = 2 * (tt % 4)
                    pv = psflat[:, b0 * 512:(b0 + 2) * 512]

                    def tr():
                        last = None
                        for kc in range(8):
                            last = pe.transpose(pv[:, kc * 128:(kc + 1) * 128], xs[s][:, kc * 128:(kc + 1) * 128], ident_f[:])
                        return last
                    op("pe", tr, reads=["xs%d" % s, "ident_f"], writes=B(b0, b0 + 1))

                    def ev_act():
                        last = None
                        for kc in (0, 1, 2, 3):
                            last = act.activation(out=hT[:, kc, :, tt * 8:(tt + 1) * 8],
                                                  in_=pv[:, kc * 128:(kc + 1) * 128].rearrange("p (j r) -> p r j", r=16),
                                                  func=AF.Identity, bias=modT[0][:, kc:kc + 1], scale=modT[0][:, 8 + kc:9 + kc])
                        return last

                    def ev_dve():
                        last = None
                        for kc in (4, 5, 6, 7):
                            last = dve.tensor_scalar(out=hT[:, kc, :, tt * 8:(tt + 1) * 8],
                                                     in0=pv[:, kc * 128:(kc + 1) * 128].rearrange("p (j r) -> p r j", r=16),
                                                     scalar1=modT[0][:, 8 + kc:9 + kc], scalar2=modT[0][:, kc:kc + 1],
                                                     op0=ALU.mult, op1=ALU.add)
                        return last
                    op("act", ev_act, reads=B(b0) + ["modT0"], writes=["hTa%d" % tt])
                    op("dve", ev_dve, reads=B(b0 + 1) + ["modT0"], writes=["hTd%d" % tt])
                S.barrier()

            with contextlib.ExitStack() as ml:
                Wg = [sbuf(ml, "Wg%d" % s, [128, 8, 3, 128], BF16) for s in range(2)]
                Wgt = sbuf(ml, "Wgt", [128, 8, 128], BF16)
                QT = sbuf(ml, "QT", [128, TOWN], BF16)
                KT = sbuf(ml, "KT", [128, 6144], BF16)
                Vt = sbuf(ml, "Vt", [128, 48, 128], BF16)
                sg = sbuf(ml, "sg", [128, TOWN], BF16)
                acc = sbuf(ml, "acc", [128, 2, TOWN], F32)
                Ssb = [[sbuf(ml, "Ssb%d%d" % (a, s), [128, 512], F32) for s in range(2)] for a in range(2)]
                PT = [[sbuf(ml, "PT%d%d" % (a, s), [128, 512], BF16) for s in range(2)] for a in range(2)]
                yst = sbuf(ml, "yst", [128, TOWN], BF16)
                vst = [sbuf(ml, "vst%d" % s, [128, 512], BF16) for s in range(2)]
                vsl = [0]

                op("pool", lambda: pool.memset(KT[:], 0.0), writes=["KTall"])
                op("pool", lambda: pool.memset(Vt[:], 0.0), writes=["Vtall"])
                op("pool", lambda: pool.memset(vst[0][:], 0.0), writes=["vst0"])
                op("pool", lambda: pool.memset(vst[1][:], 0.0), writes=["vst1"])
                S.barrier()

                awv = awin_d.rearrange("(kc p) n -> p kc n", p=128)
                its = [(hp, g) for hp in range(8) for g in range(3)]

                def load_W(it):
                    hp, g = its[it]
                    s = it % 2

                    def f():
                        r = []
                        for wh in range(3):
                            c0 = g * 3072 + wh * 1024 + hp * 128
                            r.append(pool.dma_start(out=Wg[s][:, :, wh, :], in_=awv[:, :, c0:c0 + 128]))
                        return r
                    op("pool", f, writes=["Wg%d" % s], dma="Wg%d" % s)

                def load_Wgt(hp):
                    c0 = 9216 + hp * 128
                    op("pool", lambda: pool.dma_start(out=Wgt[:], in_=awv[:, :, c0:c0 + 128]), writes=["Wgt"], dma="Wgt")

                ipb = [0]

                def next_bank():
                    b = 6 + ipb[0] % 2
                    ipb[0] += 1
                    return b

                evq = [0]

                def evac_copy(out_ap, in_ap, reads, writes):
                    evq[0] += 1
                    if evq[0] % 3 != 0:
                        op("act", lambda: act.activation(out=out_ap, in_=in_ap, func=AF.Identity), reads=reads, writes=writes)
                    else:
                        op("dve", lambda: dve.tensor_copy(out=out_ap, in_=in_ap), reads=reads, writes=writes)

                norm_q = []
                norm_done = [8]

                def norm_upto(c):
                    while norm_q and norm_done[0] <= c:
                        norm_q.pop(0)()
                        norm_done[0] += 1

                load_W(0)
                load_Wgt(0)
                if not stop_after_l0:
                    steps1, fin1 = mod_steps(1, ml, 2, banks=(6, 7, 6))
                else:
                    steps1 = []

                for it, (hp, g) in enumerate(its):
                    if steps1:
                        steps1.pop(0)()
                    d = GROUP_D[g]
                    Lq = TOWN // d
                    Lk = Lq + 64
                    Lkp = Lk + 64
                    ntr = Lkp // 128
                    ws = it % 2
                    W = Wg[ws]
                    wname = "Wg%d" % ws
                    if it + 1 < len(its):
                        load_W(it + 1)

                    km = 16 // d

                    def hrhs(kc, r, j0, n):
                        v = hT[:, kc, :, :].rearrange("p (m r) q -> p r m q", r=d)
                        return v[:, r, :, j0 // km:(j0 + n) // km]

                    def psmq(bk, o, n):
                        return ps[:, bk, o:o + n].rearrange("p (m q) -> p m q", m=km)

                    def pqm(bk, o, n):
                        return ps[:, bk, o:o + n].rearrange("p (m q) -> p q m", m=km)

                    def unperm(ap2d):
                        return ap2d.rearrange("p (q m) -> p q m", m=km)

                    def proj(wh, bk, r, j0, n, o=0):
                        def mm():
                            last = None
                            for kc in range(8):
                                last = pe.matmul(psmq(bk, o, n), lhsT=W[:, kc, wh, :], rhs=hrhs(kc, r, j0, n),
                                                 start=(kc == 0), stop=(kc == 7))
                            return last
                        op("pe", mm, reads=[wname], writes=B(bk))

                    def q_stage(r, i):
                        if Lq >= 512:
                            if i * 512 >= Lq:
                                return
                            bk = next_bank()
                            pi0 = r * Lq + i * 512
                            proj(0, bk, r, i * 512, 512)
                            evac_copy(unperm(QT[:, pi0:pi0 + 512]), pqm(bk, 0, 512), B(bk), ["QT%d" % (pi0 // 512)])
                        else:
                            if i != 0 or r % 2 != 0:
                                return
                            bk = next_bank()
                            pi0 = r * Lq

                            def mm():
                                last = None
                                for kc in range(8):
                                    last = pe.matmul(ps[:, bk, :].rearrange("p (a b) -> p a b", a=2), lhsT=W[:, kc, 0, :],
                                                     rhs=hT[:, kc, r:r + 2, 0:Lq], start=(kc == 0), stop=(kc == 7))
                                return last
                            op("pe", mm, reads=[wname], writes=B(bk))
                            evac_copy(QT[:, pi0:pi0 + 512], ps[:, bk, :], B(bk), ["QT%d" % (pi0 // 512)])

                    def k_stage(r, i):
                        j0 = i * 512
                        if j0 >= Lk:
                            return
                        n = min(512, Lk - j0)
                        bk = next_bank()
                        proj(1, bk, r, j0, n)
                        o0 = r * Lkp + 64 + j0
                        evac_copy(unperm(KT[:, o0:o0 + n]), pqm(bk, 0, n), B(bk), ["KT_%d_%d" % (r, i)])

                    def v_stage_mm(r, i):
                        c0 = 4 * i
                        if c0 >= ntr:
                            return None
                        nt = min(4, ntr - c0)
                        off = 64 if c0 == 0 else 0
                        jst = 128 * c0 - 64 + off
                        n = 128 * nt - off
                        bk = next_bank()
                        vs = vsl[0] % 2
                        vsl[0] += 1
                        proj(2, bk, r, jst, n, o=off)
                        evac_copy(unperm(vst[vs][:, off:off + n]), pqm(bk, off, n), B(bk), ["vst%d" % vs])
                        return (c0, nt, vs)

                    def v_stage_tr(r, st):
                        if st is None:
                            return
                        c0, nt, vs = st
                        bk2 = next_bank()
                        ptb = ps[:, bk2, :].bitcast(BF16)

                        def tr():
                            last = None
                            for ti in range(nt):
                                last = pe.transpose(ptb[:, ti * 128:(ti + 1) * 128], vst[vs][:, ti * 128:(ti + 1) * 128], ident_b[:])
                            return last
                        op("pe", tr, reads=["vst%d" % vs, "ident_b"], writes=B(bk2))
                        t0 = r * ntr + c0
                        evac_copy(Vt[:, t0:t0 + nt, :], ptb[:, 0:nt * 128].rearrange("p (a b) -> p a b", a=nt), B(bk2),
                                  ["V%d" % (t0 + ti) for ti in range(nt)])

                    def gate_stage(c):
                        norm_upto(c)
                        bk = next_bank()

                        def mm():
                            last = None
                            for kc in range(8):
                                last = pe.matmul(ps[:, bk, :].rearrange("p (m q) -> p m q", m=16), lhsT=Wgt[:, kc, :],
                                                 rhs=hT[:, kc, :, c * 32:(c + 1) * 32], start=(kc == 0), stop=(kc == 7))
                            return last
                        op("pe", mm, reads=["Wgt"], writes=B(bk))
                        op("act", lambda: act.activation(out=sg[:, c * 512:(c + 1) * 512].rearrange("p (q m) -> p q m", m=16),
                                                         in_=ps[:, bk, :].rearrange("p (m q) -> p q m", m=16), func=AF.Silu),
                           reads=B(bk), writes=["sg%d" % c])

                    cs = [8.0 * SLOPES[g][2 * hp + a] * d for a in range(2)]

                    def kt_chunks(r, pb):
                        lo = max(0, 256 * pb - 64) // 512
                        hi = min(Lk - 1, 256 * pb + 319) // 512
                        return ["KT_%d_%d" % (r, i) for i in range(lo, hi + 1)]

                    def emit_qk(blk):
                        r, pb, bi = blk
                        sl = bi % 2
                        qname = "QT%d" % ((r * Lq + 256 * pb) // 512)
                        for a in range(2):
                            bk = 2 * sl + a

                            def mm():
                                last = None
                                for qb2 in range(2):
                                    b = 2 * pb + qb2
                                    for kt in range(2):
                                        c = b + kt
                                        col = (qb2 * 2 + kt) * 128
                                        last = pe.matmul(ps[:, bk, col:col + 128],
                                                         lhsT=KT[64 * a:64 * a + 64, r * Lkp + 128 * c: r * Lkp + 128 * c + 128],
                                                         rhs=QT[64 * a:64 * a + 64, r * Lq + 128 * b: r * Lq + 128 * b + 128],
                                                         start=True, stop=True)
                                return last
                            op("pe", mm, reads=[qname] + kt_chunks(r, pb), writes=B(bk))

                    def emit_sm(blk):
                        r, pb, bi = blk
                        sl = bi % 2
                        bt = base_first if pb == 0 else base_reg
                        for a in range(2):
                            bk = 2 * sl + a
                            op("dve", lambda: dve.scalar_tensor_tensor(out=Ssb[a][sl][:], in0=bt[:], scalar=cs[a], in1=ps[:, bk, :],
                                                                       op0=ALU.mult, op1=ALU.add),
                               reads=B(bk), writes=["Ssb%d%d" % (a, sl)])
                            op("act", lambda: act.activation(out=PT[a][sl][:], in_=Ssb[a][sl][:], func=AF.Exp, scale=0.125),
                               reads=["Ssb%d%d" % (a, sl)], writes=["PT%d%d" % (a, sl)])

                    def emit_pv(blk):
                        r, pb, bi = blk
                        sl = bi % 2
                        bk = 4 + (bi % 2)
                        vnames = ["V%d" % (r * ntr + 2 * pb + k) for k in range(3)]

                        for a in range(2):
                            def mm():
                                last = None
                                for qb2 in range(2):
                                    b = 2 * pb + qb2
                                    for nd in range(2):
                                        for kt in range(2):
                                            tau = r * ntr + b + kt
                                            col = (qb2 * 2 + kt) * 128
                                            lh = Vt[:, tau, 64 * a:64 * a + 64] if nd == 0 else ones_bf[:, 0:64]
                                            oc = nd * 256 + qb2 * 128
                                            last = pe.matmul(ps[64 * a:64 * a + 64, bk, oc:oc + 128], lhsT=lh,
                                                             rhs=PT[a][sl][:, col:col + 128], start=(kt == 0), stop=(kt == 1),
                                                             tile_position=(0, 64 * a))
                                return last
                            op("pe", mm, reads=["PT%d%d" % (a, sl), "ones_bf"] + vnames, writes=B(bk))
                        accv = acc[:].rearrange("p n (j r) -> p n r j", r=d)[:, :, r, 256 * pb:256 * pb + 256]
                        odv = ps[:, bk, :].rearrange("p (n q) -> p n q", n=2)
                        if g == 0:
                            op("dve", lambda: dve.tensor_copy(out=accv, in_=odv), reads=B(bk), writes=["acc"])
                        else:
                            op("dve", lambda: dve.tensor_tensor(out=accv, in0=accv, in1=odv, op=ALU.add),
                               reads=B(bk) + ["acc"], writes=["acc"])

                    nst = max((Lk + 511) // 512, (ntr + 3) // 4)
                    G = []
                    for r in range(d):
                        for i in range(nst):
                            vbox = [None]

                            def g_vmm(r=r, i=i, vbox=vbox):
                                vbox[0] = v_stage_mm(r, i)

                            def g_k(r=r, i=i):
                                k_stage(r, i)

                            def g_vtr(r=r, vbox=vbox):
                                v_stage_tr(r, vbox[0])

                            def g_q(r=r, i=i):
                                q_stage(r, i)
                            for fn in (g_vmm, g_k, g_vtr, g_q):
                                G.append(((r, i), fn))
                    gates = list(range(TOWN // 512)) if g == 0 else []
                    gi = [0]

                    def emit_group():
                        key, fn = G[gi[0]]
                        gi[0] += 1
                        fn()
                        if gates and gi[0] % max(1, len(G) // 8) == 0:
                            gate_stage(gates.pop(0))

                    pending = None
                    bi = 0
                    for r in range(d):
                        for pb in range(Lq // 256):
                            req = (r, (256 * pb + 383) // 512)
                            while gi[0] < len(G) and G[gi[0]][0] <= req:
                                emit_group()
                            blk = (r, pb, bi)
                            bi += 1
                            if g == 0:
                                norm_upto(pb // 2)
                            else:
                                norm_upto(7)
                            emit_qk(blk)
                            emit_sm(blk)
                            if gi[0] < len(G):
                                emit_group()
                            if pending is not None:
                                emit_pv(pending)
                            pending = blk
                    while gi[0] < len(G):
                        emit_group()
                    if pending is not None:
                        emit_pv(pending)
                    if g == 0:
                        while gates:
                            gate_stage(gates.pop(0))
                        if hp + 1 < 8:
                            load_Wgt(hp + 1)

                    if g == 2:
                        def mk_piece(c, hp=hp):
                            def piece():
                                sl_ = slice(c * 512, (c + 1) * 512)
                                op("dve", lambda: dve.reciprocal(out=acc[:, 1, sl_], in_=acc[:, 1, sl_]), reads=["acc"], writes=["acc"])
                                op("dve", lambda: dve.tensor_tensor(out=acc[:, 0, sl_], in0=acc[:, 0, sl_], in1=acc[:, 1, sl_],
                                                                    op=ALU.mult), reads=["acc"], writes=["acc"])
                                op("dve", lambda: dve.tensor_tensor(out=yst[:, sl_], in0=acc[:, 0, sl_], in1=sg[:, sl_], op=ALU.mult),
                                   reads=["acc", "sg%d" % c], writes=["yst"])
                                if c == 7:
                                    op("sp", lambda: sp.dma_start(out=yts_d[:, hp, :], in_=yst[:]), reads=["yst"], dma="yst")
                            return piece
                        norm_q.extend(mk_piece(c) for c in range(8))
                        norm_done[0] = 0
                norm_upto(7)
                S.barrier()
        S.barrier()

        if not stop_after_l0:
            Wib = sbuf(es, "Wib", [128, 8, 6144], BF16)
            Wob = sbuf(es, "Wob", [128, 16, D], BF16)
            wsT = sbuf(es, "wsT", [128, 16, 128], BF16)
        with contextlib.ExitStack() as lo:
            WoA = sbuf(lo, "WoA", [128, 8, D], BF16)
            op("pool", lambda: pool.dma_start(out=WoA[:], in_=awout_d.rearrange("(kc p) n -> p kc n", p=128)),
               writes=["WoA"], dma="WoA")
            phase_params(0, lo)
            for kc in range(8):
                op("dve" if kc % 2 == 0 else "pool",
                   (lambda kc=kc: (dve if kc % 2 == 0 else pool).tensor_tensor(out=WoA[:, kc, :], in0=WoA[:, kc, :],
                                                                                in1=PH["gate_bc"][:], op=ALU.mult)),
                   reads=["WoA", "gate_bc"], writes=["WoA_g%d" % kc])
            S.barrier()
            if not stop_after_l0:
                bwv = bwin_d.rearrange("(kc p) n -> p kc n", p=128)
                for kc in range(8):
                    op("pool", lambda: pool.dma_start(out=Wib[:, kc, :], in_=bwv[:, kc, :]), writes=["Wib%d" % kc], dma="Wib%d" % kc)
                bov = bwout_d.rearrange("(kc p) n -> p kc n", p=128)
                for q in range(4):
                    op("pool", lambda: pool.dma_start(out=Wob[:, 4 * q:4 * q + 4, :], in_=bov[:, 4 * q:4 * q + 4, :]),
                       writes=["Wob%d" % q], dma="Wob%d" % q)
                op("pool", lambda: pool.dma_start(out=wsT[:], in_=wsT_d[:, :, :]), writes=["wsT"], dma="wsT")
            yin = [sbuf(lo, "yin%d" % s, [128, 8, 512], BF16) for s in range(2)]
            xr = [sbuf(lo, "xr%d" % s, [128, D], F32) for s in range(2)]
            ot = [sbuf(lo, "ot%d" % s, [128, D], F32) for s in range(2)]
            def lo_loads(tt):
                q4, ys = tt // 4, (tt // 4) % 2
                if tt % 4 == 0:
                    op("sp", lambda: sp.dma_start(out=yin[ys][:], in_=yts_d[:, :, q4 * 512:(q4 + 1) * 512]),
                       writes=["yin%d" % ys], dma="yin%d" % ys)
                s = tt % 2
                op("sp", lambda: sp.dma_start(out=xr[s][:], in_=x_d[tt * 128:(tt + 1) * 128, :]),
                   writes=["xr%d" % s], dma="xr%d" % s)
            STs = []
            for q in range(2):
                STs.append({"stat": sbuf(lo, "lstat%d" % q, [128, 2, 6], F32), "mv": sbuf(lo, "lmv%d" % q, [128, 2], F32),
                            "rstd": sbuf(lo, "lrstd%d" % q, [128, 1], F32), "nmr": sbuf(lo, "lnmr%d" % q, [128, 1], F32),
                            "sfx": "_l%d" % q})
            NT = TOWN // 128

            def lo_E1(tt):
                q4, ys = tt // 4, (tt // 4) % 2
                s = tt % 2
                b0 = 2 * (tt % 4)

                def mm():
                    last = None
                    for hf in range(2):
                        for kc in range(8):
                            last = pe.matmul(ps[:, b0 + hf, :], lhsT=yin[ys][:, kc, (tt % 4) * 128:(tt % 4 + 1) * 128],
                                             rhs=WoA[:, kc, hf * 512:(hf + 1) * 512], start=(kc == 0), stop=(kc == 7))
                    return last
                op("pe", mm, reads=["yin%d" % ys, "WoA"], writes=B(b0, b0 + 1))
                epi_E1(b0, xr[s][:], "xr%d" % s, ot[s], "ot%d" % s, STs[s])

            def lo_E2(tt):
                s = tt % 2
                epi_E2(ot[s], "ot%d" % s, x1s_d[tt * 128:(tt + 1) * 128, :], "ot%d" % s, STs[s], gb="dve")

            lo_loads(0)
            lo_loads(1)
            lo_E1(0)
            for tt in range(NT):
                if tt + 1 < NT:
                    lo_E1(tt + 1)
                lo_E2(tt)
                if tt + 2 < NT:
                    lo_loads(tt + 2)
            S.barrier()
        S.barrier()

        if not stop_after_l0:
            with contextlib.ExitStack() as l1:
                phase_params(1, l1)
                for kc in range(16):
                    op("dve" if kc % 2 == 0 else "pool",
                       (lambda kc=kc: (dve if kc % 2 == 0 else pool).tensor_tensor(out=Wob[:, kc, :], in0=Wob[:, kc, :],
                                                                                    in1=PH["gate_bc"][:], op=ALU.mult)),
                       reads=["gate_bc"], writes=["Wob_g%d" % kc])
                Cb = sbuf(l1, "Cb", [128, 16, 128], F32)
                lng_bc = sbuf(l1, "lng_bc", [128, 2048], F32)
                op("sp", lambda: sp.dma_start(out=lng_bc[:], in_=blg_d[0:1, :].partition_broadcast(128)),
                   writes=["lng_bc"], dma="lng")
                with contextlib.ExitStack() as s1:
                    lnb_t = sbuf(s1, "lnb_t", [128, 16, 128], F32)
                    wsN = sbuf(s1, "wsN", [128, 16, 128], F32)
                    bsT = sbuf(s1, "bsT", [128, 16], F32)
                    rsw = sbuf(s1, "rsw", [128, 16], F32)
                    op("sp", lambda: sp.dma_start(out=lnb_t[:].rearrange("p a b -> p (a b)"),
                                                  in_=blb_d[0:1, :].partition_broadcast(128)), writes=["lnb_t"], dma="lnb")
                    op("sp", lambda: sp.dma_start(out=wsN[:], in_=wsN_d[:, :, :]), writes=["wsN"], dma="wsN")
                    op("sp", lambda: sp.dma_start(out=bsT[:], in_=bsT_d[:, :]), writes=["bsT"], dma="bsT")
                    op("dve", lambda: dve.reduce_sum(out=rsw[:], in_=wsN[:], axis=mybir.AxisListType.X),
                       reads=["wsN"], writes=["rsw"])
                    op("dve", lambda: dve.tensor_tensor(out=Cb[:], in0=lnb_t[:], in1=rsw[:].unsqueeze(2).to_broadcast([128, 16, 128]),
                                                        op=ALU.mult), reads=["lnb_t", "rsw"], writes=["Cb"])
                    op("dve", lambda: dve.tensor_tensor(out=Cb[:], in0=Cb[:], in1=bsT[:].unsqueeze(2).to_broadcast([128, 16, 128]),
                                                        op=ALU.add), reads=["Cb", "bsT"], writes=["Cb"])
                    S.barrier()
                Cflat = Cb[:].rearrange("p a b -> p (a b)")

                x1t = [sbuf(l1, "x1t%d" % s, [128, D], F32) for s in range(2)]
                h1Ts = [sbuf(l1, "h1T%d" % q, [128, 8, 128], BF16) for q in range(2)]
                vf = sbuf(l1, "vf", [128, 2048], F32)
                vhat = sbuf(l1, "vhat", [128, 2048], BF16)
                ubf = sbuf(l1, "ubf", [128, 2048], BF16)
                sgb = sbuf(l1, "sgb", [128, 2048], BF16)
                yT = sbuf(l1, "yT", [128, 16, 128], BF16)
                o1 = sbuf(l1, "o1", [128, D], F32)
                wib_all = ["Wib%d" % kc for kc in range(8)]
                wob_all = ["Wob%d" % q for q in range(4)]
                NCH = TOWN // 128

                stat2 = sbuf(l1, "stat2", [128, 4, 6], F32)
                mv2 = sbuf(l1, "mv2", [128, 2], F32)
                rstd2 = sbuf(l1, "rstd2", [128, 1], F32)
                nmr2 = sbuf(l1, "nmr2", [128, 1], F32)
                vfn = ["vf%d" % j for j in range(4)]
                ubn = ["ubf%d" % j for j in range(4)]
                sgn1 = ["sgb%d" % j for j in range(4)]
                rot = [0]

                def load_x1(tt):
                    s = tt % 2
                    op("sp", lambda: sp.dma_start(out=x1t[s][:], in_=x1s_d[tt * 128:(tt + 1) * 128, :]),
                       writes=["x1t%d" % s], dma="x1t%d" % s)

                def stage_A(tt):
                    s = tt % 2
                    h1T = h1Ts[s]
                    pv = psflat[:, 0:1024]

                    def tr():
                        last = None
                        for kc in range(8):
                            last = pe.transpose(pv[:, kc * 128:(kc + 1) * 128], x1t[s][:, kc * 128:(kc + 1) * 128], ident_f[:])
                        return last
                    op("pe", tr, reads=["x1t%d" % s, "ident_f"], writes=B(0, 1))

                    def ev_act():
                        last = None
                        for kc in range(8):
                            last = act.activation(out=h1T[:, kc, :], in_=pv[:, kc * 128:(kc + 1) * 128], func=AF.Identity,
                                                  bias=modT[1][:, kc:kc + 1], scale=modT[1][:, 8 + kc:9 + kc])
                        return last
                    op("act", ev_act, reads=B(0, 1) + ["modT1"], writes=["h1T%d" % s])

                def inproj_chunk(tt, cc):
                    bk = rot[0] % 4
                    rot[0] += 1
                    h1T = h1Ts[tt % 2]

                    def mm():
                        last = None
                        for kc in range(8):
                            last = pe.matmul(ps[:, bk, :], lhsT=h1T[:, kc, :], rhs=Wib[:, kc, cc * 512:(cc + 1) * 512],
                                             start=(kc == 0), stop=(kc == 7))
                        return last
                    op("pe", mm, reads=["h1T%d" % (tt % 2)] + wib_all, writes=B(bk))
                    return bk

                def stage_B1a(tt):
                    for j in range(4):
                        bk = inproj_chunk(tt, 4 + j)
                        op("act", lambda: act.activation(out=vf[:, j * 512:(j + 1) * 512], in_=ps[:, bk, :], func=AF.Gelu),
                           reads=B(bk), writes=["vf%d" % j])

                def stage_B1b(tt):
                    for j in range(4):
                        op("dve", lambda: dve.bn_stats(out=stat2[:, j, :], in_=vf[:, j * 512:(j + 1) * 512]),
                           reads=["vf%d" % j], writes=["stat2_%d" % j])
                    op("dve", lambda: dve.bn_aggr(out=mv2[:], in_=stat2[:].rearrange("p a b -> p (a b)")),
                       reads=["stat2_%d" % j for j in range(4)], writes=["mv2"])
                    op("act", lambda: act.activation(out=rstd2[:], in_=mv2[:, 1:2], func=AF.Sqrt, bias=eps_t[:], scale=1.0),
                       reads=["mv2", "eps_t"], writes=["rstd2"])
                    op("dve", lambda: dve.reciprocal(out=rstd2[:], in_=rstd2[:]), reads=["rstd2"], writes=["rstd2"])
                    op("dve", lambda: dve.scalar_tensor_tensor(out=nmr2[:], in0=mv2[:, 0:1], scalar=-1.0, in1=rstd2[:],
                                                               op0=ALU.mult, op1=ALU.mult),
                       reads=["mv2", "rstd2"], writes=["nmr2"])
                    op("act", lambda: act.activation(out=vhat[:], in_=vf[:], func=AF.Identity, bias=nmr2[:], scale=rstd2[:]),
                       reads=vfn + ["rstd2", "nmr2"], writes=["vhat"])

                def stage_B2(tt, js):
                    for j8 in js:
                        if j8 < 4:
                            j = j8
                            bk = inproj_chunk(tt, 8 + j)
                            op("act", lambda: act.activation(out=sgb[:, j * 512:(j + 1) * 512], in_=ps[:, bk, :], func=AF.Silu),
                               reads=B(bk), writes=["sgb%d" % j])
                        else:
                            j = j8 - 4
                            bk = inproj_chunk(tt, j)
                            op("act", lambda: act.activation(out=ubf[:, j * 512:(j + 1) * 512], in_=ps[:, bk, :], func=AF.Gelu),
                               reads=B(bk), writes=["ubf%d" % j])

                def stage_C(tt):
                    def sp_mm():
                        last = None
                        for gi in range(16):
                            last = pe.matmul(ps[:, 4 + gi // 4, (gi % 4) * 128:(gi % 4 + 1) * 128], lhsT=wsT[:, gi, :],
                                             rhs=vhat[:, gi * 128:(gi + 1) * 128], start=True, stop=True)
                        return last
                    op("pe", sp_mm, reads=["vhat", "wsT"], writes=B(4, 5, 6, 7))
                    svp = psflat[:, 4 * 512:8 * 512]
                    op("dve", lambda: dve.tensor_tensor(out=vf[:], in0=svp, in1=lng_bc[:], op=ALU.mult),
                       reads=B(4, 5, 6, 7) + ["lng_bc"] + vfn, writes=vfn)
                    op("dve", lambda: dve.tensor_tensor(out=vf[:], in0=vf[:], in1=Cflat, op=ALU.add), reads=vfn + ["Cb"], writes=vfn)
                    op("dve", lambda: dve.tensor_tensor(out=vf[:], in0=vf[:], in1=ubf[:], op=ALU.mult),
                       reads=vfn + ubn, writes=vfn)
                    op("dve", lambda: dve.tensor_tensor(out=ubf[:], in0=vf[:], in1=sgb[:], op=ALU.mult),
                       reads=vfn + sgn1 + ubn, writes=ubn)

                def stage_D(tt):
                    ytp = psflat[:, 4 * 512:6 * 512].bitcast(BF16)

                    def ytr():
                        last = None
                        for gi in range(16):
                            last = pe.transpose(ytp[:, gi * 128:(gi + 1) * 128], ubf[:, gi * 128:(gi + 1) * 128], ident_b[:])
                        return last
                    op("pe", ytr, reads=ubn + ["ident_b"], writes=B(4, 5))
                    op("dve", lambda: dve.tensor_copy(out=yT[:].rearrange("p a b -> p (a b)"), in_=ytp),
                       reads=B(4, 5), writes=["yT"])

                def stage_E(tt):
                    s = tt % 2

                    def omm():
                        last = None
                        for hf in range(2):
                            for kc in range(16):
                                last = pe.matmul(ps[:, 6 + hf, :], lhsT=yT[:, kc, :], rhs=Wob[:, kc, hf * 512:(hf + 1) * 512],
                                                 start=(kc == 0), stop=(kc == 15))
                        return last
                    op("pe", omm, reads=["yT"] + wob_all, writes=B(6, 7))
                    epilogue(6, x1t[s][:], "x1t%d" % s, o1, "o1", out_d[tt * 128:(tt + 1) * 128, :], "o1")

                load_x1(0)
                load_x1(1)
                stage_A(0)
                stage_B1a(0)
                stage_B1b(0)
                stage_B2(0, range(8))
                stage_A(1)
                for tt in range(NCH):
                    nx = tt + 1 < NCH
                    stage_C(tt)
                    if nx:
                        stage_B1a(tt + 1)
                    stage_D(tt)
                    if nx:
                        stage_B1b(tt + 1)
                        stage_B2(tt + 1, range(0, 2))
                    stage_E(tt)
                    if tt + 2 < NCH:
                        load_x1(tt + 2)
                    if nx:
                        stage_B2(tt + 1, range(2, 8))
                    if tt + 2 < NCH:
                        stage_A(tt + 2)
                S.barrier()
        S.barrier()
    return nc


def _prep_inputs(inputs):
    x = np.asarray(inputs["x"], dtype=np.float32)
    c = np.asarray(inputs["c"], dtype=np.float32)
    w_s = np.asarray(inputs["b_w_s"], dtype=np.float32)[0]
    b_s = np.asarray(inputs["b_b_s"], dtype=np.float32)[0]
    shared = {
        "ada_w": np.ascontiguousarray(inputs["ada_w"], dtype=np.float32),
        "ada_b": np.ascontiguousarray(inputs["ada_b"], dtype=np.float32),
        "post_ln_g": np.ascontiguousarray(inputs["post_ln_g"], dtype=np.float32),
        "post_ln_b": np.ascontiguousarray(inputs["post_ln_b"], dtype=np.float32),
        "a_w_in": np.ascontiguousarray(np.asarray(inputs["a_w_in"], dtype=np.float32)[0]),
        "a_w_out": np.ascontiguousarray(np.asarray(inputs["a_w_out"], dtype=np.float32)[0]),
        "b_w_in": np.ascontiguousarray(np.asarray(inputs["b_w_in"], dtype=np.float32)[0]),
        "b_ln_g": np.ascontiguousarray(np.asarray(inputs["b_ln_g"], dtype=np.float32)[0:1]),
        "b_ln_b": np.ascontiguousarray(np.asarray(inputs["b_ln_b"], dtype=np.float32)[0:1]),
        "b_w_out": np.ascontiguousarray(np.asarray(inputs["b_w_out"], dtype=np.float32)[0]),
    }
    in_maps = []
    for core in range(8):
        b, half = core // 2, core % 2
        if half == 0:
            xl = x[b, 0:TLOC]
            ws, bs = w_s, b_s
        else:
            xl = x[b, ::-1][0:TLOC]
            ws, bs = w_s[:, ::-1, ::-1], b_s[:, ::-1]
        m = dict(shared)
        m["x"] = np.ascontiguousarray(xl)
        m["ct"] = np.ascontiguousarray(c[b].reshape(8, 128).T)
        m["w_sT"] = np.ascontiguousarray(ws.transpose(2, 0, 1))
        m["w_sN"] = np.ascontiguousarray(ws.transpose(1, 0, 2))
        m["b_sT"] = np.ascontiguousarray(bs.T)
        in_maps.append(m)
    return in_maps


def kernel(**inputs):
    in_maps = _prep_inputs(inputs)
    nc = build()
    res = run_bass_kernel_spmd(nc, in_maps, core_ids=list(range(8)))
    out = np.empty((4, SEQ, D), dtype=np.float32)
    for core in range(8):
        b, half = core // 2, core % 2
        o = np.asarray(res.results[core]["out"], dtype=np.float32)
        if half == 0:
            out[b, 0:TOWN] = o
        else:
            out[b, TOWN:SEQ] = o[::-1]
    return out
```

```python
import contextlib
import numpy as np
import concourse.bass as bass
import concourse.mybir as mybir
from concourse.bass_utils import run_bass_kernel_spmd

F32 = mybir.dt.float32
BF16 = mybir.dt.bfloat16
AF = mybir.ActivationFunctionType
ALU = mybir.AluOpType

D = 1024
SEQ = 8192
TOWN = 4096
TLOC = 5120
NEGBIG = -1.0e6
ALPHA = float((2 * 2) ** 0.25)
EPS = 1e-5
GROUP_D = (1, 4, 16)
SLOPES = [[2.0 ** (-8.0 * (g * 16 + h + 1) / 48.0) for h in range(16)] for g in range(3)]


class Sched:
    def __init__(self, nc, es):
        self.nc = nc
        self.es = es
        self.eng = {"pe": nc.tensor, "act": nc.scalar, "dve": nc.vector, "pool": nc.gpsimd, "sp": nc.sync}
        self.sems = {}
        self.count = {}
        for k in ("pe", "act", "dve", "pool"):
            self.sems[k] = es.enter_context(nc.semaphore("sem_" + k))
            self.count[k] = 0
        self.waited = {k: {} for k in self.eng}
        self.last_w = {}
        self.readers = {}

    def _dsem(self, key):
        if key not in self.sems:
            self.sems[key] = self.es.enter_context(self.nc.semaphore("dsem_" + key))
            self.count[key] = 0
        return self.sems[key]

    def _wait(self, eng, ev):
        key, val = ev
        if self.waited[eng].get(key, 0) >= val:
            return
        self.eng[eng].wait_ge(self.sems[key], val)
        self.waited[eng][key] = val

    def op(self, eng, fn, reads=(), writes=(), dma=None):
        deps = set()
        for r in reads:
            ev = self.last_w.get(r)
            if ev is not None:
                deps.add(ev)
        for w in writes:
            ev = self.last_w.get(w)
            if ev is not None:
                deps.add(ev)
            for ev in self.readers.get(w, ()):
                deps.add(ev)
        for ev in deps:
            if eng == "pe" and ev[0] == "pe":
                continue
            self._wait(eng, ev)
        res = fn()
        if dma is not None:
            sem = self._dsem(dma)
            insts = res if isinstance(res, (list, tuple)) else [res]
            for ins in insts:
                ins.then_inc(sem, 16)
            self.count[dma] += 16 * len(insts)
            ev = (dma, self.count[dma])
        else:
            res.then_inc(self.sems[eng], 1)
            self.count[eng] += 1
            ev = (eng, self.count[eng])
        for w in writes:
            self.last_w[w] = ev
            self.readers[w] = []
        for r in reads:
            self.readers.setdefault(r, []).append(ev)
        return ev

    def barrier(self):
        for e in self.eng:
            for key, cnt in self.count.items():
                if cnt > 0:
                    self._wait(e, (key, cnt))
        self.last_w = {}
        self.readers = {}


def build(stop_after_l0=False):
    nc = bass.Bass("TRN2", target_bir_lowering=False)

    def din(name, shape, dt=F32):
        return nc.dram_tensor(name, shape, dt, kind="ExternalInput").ap()

    x_d = din("x", [TLOC, D])
    ct_d = din("ct", [128, 8])
    adaw_d = din("ada_w", [2, D, 3 * D])
    adab_d = din("ada_b", [2, 3 * D])
    plg_d = din("post_ln_g", [2, D])
    plb_d = din("post_ln_b", [2, D])
    awin_d = din("a_w_in", [D, 10240])
    awout_d = din("a_w_out", [D, D])
    bwin_d = din("b_w_in", [D, 6144])
    blg_d = din("b_ln_g", [1, 2048])
    blb_d = din("b_ln_b", [1, 2048])
    wsT_d = din("w_sT", [128, 16, 128])
    wsN_d = din("w_sN", [128, 16, 128])
    bsT_d = din("b_sT", [128, 16])
    bwout_d = din("b_w_out", [2048, D])
    yts_d = nc.dram_tensor("yts", [128, 8, TOWN], BF16, kind="Internal").ap()
    if stop_after_l0:
        x1s_d = nc.dram_tensor("x1s", [TOWN, D], F32, kind="ExternalOutput").ap()
        out_d = None
    else:
        x1s_d = nc.dram_tensor("x1s", [TOWN, D], F32, kind="Internal").ap()
        out_d = nc.dram_tensor("out", [TOWN, D], F32, kind="ExternalOutput").ap()

    with contextlib.ExitStack() as es:
        S = Sched(nc, es)
        op = S.op
        pe, act, dve, pool, sp = nc.tensor, nc.scalar, nc.vector, nc.gpsimd, nc.sync

        def sbuf(stack, name, shape, dt):
            return stack.enter_context(nc.sbuf_tensor(name, shape, dt))

        ps = es.enter_context(nc.psum_tensor("ps", [128, 8, 512], F32))
        psflat = ps[:].rearrange("p b n -> p (b n)")

        def bank(b):
            return ps[:, b, :]

        def B(*bs):
            return ["ps%d" % b for b in bs]

        ident_f = sbuf(es, "ident_f", [128, 128], F32)
        ident_b = sbuf(es, "ident_b", [128, 128], BF16)
        ones_bf = sbuf(es, "ones_bf", [128, 64], BF16)
        ones_row = sbuf(es, "ones_row", [1, 128], F32)
        cond = sbuf(es, "cond", [128, 8], F32)
        modT = [sbuf(es, "modT%d" % i, [128, 24], F32) for i in range(2)]
        PH = {}
        stat = sbuf(es, "stat", [128, 4, 6], F32)
        mv = sbuf(es, "mv", [128, 2], F32)
        rstd = sbuf(es, "rstd", [128, 1], F32)
        nmr = sbuf(es, "nmr", [128, 1], F32)
        eps_t = sbuf(es, "eps_t", [128, 1], F32)

        def mk_ident():
            pool.memset(ident_f[:], 0.0)
            return pool.affine_select(out=ident_f[:], in_=ident_f[:], pattern=[[-1, 128]],
                                      compare_op=ALU.not_equal, fill=1.0, base=0, channel_multiplier=1)
        op("pool", mk_ident, writes=["ident_f"])
        op("dve", lambda: dve.tensor_copy(out=ident_b[:], in_=ident_f[:]), reads=["ident_f"], writes=["ident_b"])
        op("pool", lambda: pool.memset(ones_bf[:], 1.0), writes=["ones_bf"])
        op("pool", lambda: pool.memset(ones_row[:], 1.0), writes=["ones_row"])
        op("pool", lambda: pool.memset(eps_t[:], EPS), writes=["eps_t"])

        ct_sb = sbuf(es, "ct_sb", [128, 8], F32)
        op("sp", lambda: sp.dma_start(out=ct_sb[:], in_=ct_d[:, :]), writes=["ct_sb"], dma="ct")
        op("act", lambda: act.activation(out=cond[:], in_=ct_sb[:], func=AF.Silu), reads=["ct_sb"], writes=["cond"])

        def phase_params(i, stack):
            PH["gate_bc"] = sbuf(stack, "gate_bc%d" % i, [128, D], F32)
            PH["plg_bc"] = sbuf(stack, "plg_bc%d" % i, [128, D], F32)
            PH["plb_bc"] = sbuf(stack, "plb_bc%d" % i, [128, D], F32)
            gate_bc, plg_bc, plb_bc = PH["gate_bc"], PH["plg_bc"], PH["plb_bc"]
            op("sp", lambda: sp.dma_start(out=plg_bc[:], in_=plg_d[i:i + 1, :].partition_broadcast(128)),
               writes=["plg_bc"], dma="plg")
            op("sp", lambda: sp.dma_start(out=plb_bc[:], in_=plb_d[i:i + 1, :].partition_broadcast(128)),
               writes=["plb_bc"], dma="plb")
            with contextlib.ExitStack() as tmp:
                ones_f = sbuf(tmp, "ones_f%d" % i, [128, 128], F32)
                dg = sbuf(tmp, "dg%d" % i, [128, 8, 128], F32)
                op("pool", lambda: pool.memset(ones_f[:], 1.0), writes=["ones_f"])

                def mkd():
                    last = None
                    for j in range(8):
                        last = dve.tensor_scalar(out=dg[:, j, :], in0=ident_f[:], scalar1=modT[i][:, 16 + j:17 + j],
                                                 scalar2=None, op0=ALU.mult)
                    return last
                op("dve", mkd, reads=["ident_f", "modT%d" % i], writes=["dg"])

                def gbc():
                    last = None
                    for j in range(8):
                        last = pe.matmul(ps[:, 4 + j // 4, (j % 4) * 128:(j % 4 + 1) * 128], lhsT=ones_f[:], rhs=dg[:, j, :],
                                         start=True, stop=True)
                    return last
                op("pe", gbc, reads=["ones_f", "dg"], writes=B(4, 5))
                op("dve", lambda: dve.tensor_copy(out=gate_bc[:], in_=psflat[:, 4 * 512:6 * 512]),
                   reads=B(4, 5), writes=["gate_bc"])
                S.barrier()

        def mod_steps(i, stack, nslots, banks=(0, 1, 2)):
            stg = [sbuf(stack, "adastg%d_%d" % (i, s), [128, 8, 128], F32) for s in range(nslots)]
            adab_sb = [sbuf(stack, "adab_sb%d_%d" % (i, s), [1, 128], F32) for s in range(nslots)]
            modrow = [sbuf(stack, "modrow%d_%d" % (i, s), [1, 128], F32) for s in range(2)]
            adav = adaw_d[i].rearrange("(kc p) n -> p kc n", p=128)

            def load(j):
                s = j % nslots
                op("sp", lambda: sp.dma_start(out=stg[s][:], in_=adav[:, :, j * 128:(j + 1) * 128]),
                   writes=["adastg%d" % s], dma="adastg%d" % s)
                op("sp", lambda: sp.dma_start(out=adab_sb[s][:], in_=adab_d[i:i + 1, j * 128:(j + 1) * 128]),
                   writes=["adab_sb%d" % s], dma="adab%d" % s)

            def step(j):
                if j == 0:
                    for jj in range(min(nslots, 24)):
                        load(jj)
                s = j % nslots
                bk = banks[j % 2]
                ms = j % 2
                bc = banks[2]

                def mm():
                    last = None
                    for kc in range(8):
                        last = pe.matmul(ps[0:1, bk, 0:128], lhsT=cond[:, kc:kc + 1], rhs=stg[s][:, kc, :],
                                         start=(kc == 0), stop=(kc == 7))
                    return last
                op("pe", mm, reads=["cond", "adastg%d" % s], writes=B(bk))
                plus = 1.0 if 8 <= j < 16 else 0.0
                op("dve", lambda: dve.scalar_tensor_tensor(out=modrow[ms][:], in0=ps[0:1, bk, 0:128], scalar=plus,
                                                           in1=adab_sb[s][:], op0=ALU.add, op1=ALU.add),
                   reads=B(bk) + ["adab_sb%d" % s], writes=["modrow%d" % ms])
                op("pe", lambda: pe.matmul(ps[:, bc, 0:1], lhsT=modrow[ms][0:1, :], rhs=ones_row[0:1, 0:1],
                                           start=True, stop=True),
                   reads=["modrow%d" % ms, "ones_row"], writes=B(bc))
                op("dve", lambda: dve.tensor_copy(out=modT[i][:, j:j + 1], in_=ps[:, bc, 0:1]), reads=B(bc),
                   writes=["modT%d_%d" % (i, j)])
                if j + nslots < 24:
                    load(j + nslots)

            def fin():
                pass
            return [(lambda j=j: step(j)) for j in range(24)], fin

        ST0 = {"stat": stat, "mv": mv, "rstd": rstd, "nmr": nmr, "sfx": ""}

        def epi_E1(yo_b0, xres_ap, xres_name, o_tile, o_name, T):
            sfx = T["sfx"]
            yo = psflat[:, yo_b0 * 512:(yo_b0 + 2) * 512]
            op("dve", lambda: dve.scalar_tensor_tensor(out=o_tile[:], in0=xres_ap, scalar=ALPHA, in1=yo,
                                                       op0=ALU.mult, op1=ALU.add),
               reads=B(yo_b0, yo_b0 + 1) + [xres_name], writes=[o_name])

            def st():
                dve.bn_stats(out=T["stat"][:, 0, :], in_=o_tile[:, 0:512])
                return dve.bn_stats(out=T["stat"][:, 1, :], in_=o_tile[:, 512:1024])
            op("dve", st, reads=[o_name], writes=["stat0" + sfx, "stat1" + sfx])
            op("dve", lambda: dve.bn_aggr(out=T["mv"][:], in_=T["stat"][:, 0:2, :].rearrange("p a b -> p (a b)")),
               reads=["stat0" + sfx, "stat1" + sfx], writes=["mv" + sfx])
            op("act", lambda: act.activation(out=T["rstd"][:], in_=T["mv"][:, 1:2], func=AF.Sqrt, bias=eps_t[:], scale=1.0),
               reads=["mv" + sfx, "eps_t"], writes=["rstd" + sfx])

        def epi_E2(o_tile, o_name, dst_ap, dst_key, T, gb="pool"):
            sfx = T["sfx"]
            op("dve", lambda: dve.reciprocal(out=T["rstd"][:], in_=T["rstd"][:]), reads=["rstd" + sfx], writes=["rstd" + sfx])
            op("dve", lambda: dve.scalar_tensor_tensor(out=T["nmr"][:], in0=T["mv"][:, 0:1], scalar=-1.0, in1=T["rstd"][:],
                                                       op0=ALU.mult, op1=ALU.mult),
               reads=["mv" + sfx, "rstd" + sfx], writes=["nmr" + sfx])
            op("act", lambda: act.activation(out=o_tile[:], in_=o_tile[:], func=AF.Identity, bias=T["nmr"][:],
                                             scale=T["rstd"][:]),
               reads=[o_name, "rstd" + sfx, "nmr" + sfx], writes=[o_name])
            g_eng = "pool" if gb == "pool" else "dve"
            ge = pool if g_eng == "pool" else dve
            op(g_eng, lambda: ge.tensor_tensor(out=o_tile[:], in0=o_tile[:], in1=PH["plg_bc"][:], op=ALU.mult),
               reads=[o_name, "plg_bc"], writes=[o_name])
            op("pool", lambda: pool.tensor_tensor(out=o_tile[:], in0=o_tile[:], in1=PH["plb_bc"][:], op=ALU.add),
               reads=[o_name, "plb_bc"], writes=[o_name])
            op("pool", lambda: pool.dma_start(out=dst_ap, in_=o_tile[:]), reads=[o_name], dma=dst_key)

        def epilogue(yo_b0, xres_ap, xres_name, o_tile, o_name, dst_ap, dst_key, gb="pool"):
            epi_E1(yo_b0, xres_ap, xres_name, o_tile, o_name, ST0)
            epi_E2(o_tile, o_name, dst_ap, dst_key, ST0, gb)

        with contextlib.ExitStack() as l0:
            hT = sbuf(l0, "hT", [128, 8, 16, TLOC // 16], BF16)
            base_reg = sbuf(l0, "base_reg", [128, 512], F32)
            base_first = sbuf(l0, "base_first", [128, 512], F32)

            with contextlib.ExitStack() as st0:
                tmpb = sbuf(st0, "tmpb", [128, 512], F32)
                op("pool", lambda: pool.iota(base_reg[:], pattern=[[0, 2], [128, 2], [-1, 128]], base=-64,
                                             channel_multiplier=1, allow_small_or_imprecise_dtypes=True),
                   writes=["base_reg"])
                op("dve", lambda: dve.scalar_tensor_tensor(out=base_reg[:], in0=base_reg[:], scalar=-1.0, in1=base_reg[:],
                                                           op0=ALU.mult, op1=ALU.min),
                   reads=["base_reg"], writes=["base_reg"])
                op("dve", lambda: dve.tensor_scalar(out=tmpb[:], in0=base_reg[:], scalar1=-64.0, scalar2=-NEGBIG,
                                                    op0=ALU.is_lt, op1=ALU.mult), reads=["base_reg"], writes=["tmpb"])
                op("dve", lambda: dve.tensor_tensor(out=base_reg[:], in0=base_reg[:], in1=tmpb[:], op=ALU.subtract),
                   reads=["base_reg", "tmpb"], writes=["base_reg"])
                op("dve", lambda: dve.tensor_copy(out=base_first[:], in_=base_reg[:]), reads=["base_reg"],
                   writes=["base_first"])
                op("dve", lambda: dve.memset(base_first[0:64, 0:128], NEGBIG), reads=["base_first"], writes=["base_first"])

                with contextlib.ExitStack() as cm0:
                    steps0, fin0 = mod_steps(0, cm0, 6)
                    for st in steps0:
                        st()
                    fin0()
                    S.barrier()

                xs = [sbuf(st0, "xs%d" % s, [128, D], F32) for s in range(3)]
                for tt in range(TLOC // 128):
                    s = tt % 3
                    op("sp", lambda: sp.dma_start(out=xs[s][:], in_=x_d[tt * 128:(tt + 1) * 128, :]),
                       writes=["xs%d" % s], dma="xs%d" % s)
                    b0 = 2 * (tt % 4)
                    pv = psflat[:, b0 * 512:(b0 + 2) * 512]

                    def tr():
                        last = None
                        for kc in range(8):
                            last = pe.transpose(pv[:, kc * 128:(kc + 1) * 128], xs[s][:, kc * 128:(kc + 1) * 128], ident_f[:])
                        return last
                    op("pe", tr, reads=["xs%d" % s, "ident_f"], writes=B(b0, b0 + 1))

                    def ev_act():
                        last = None
                        for kc in (0, 1, 2, 3):
                            last = act.activation(out=hT[:, kc, :, tt * 8:(tt + 1) * 8],
                                                  in_=pv[:, kc * 128:(kc + 1) * 128].rearrange("p (j r) -> p r j", r=16),
                                                  func=AF.Identity, bias=modT[0][:, kc:kc + 1], scale=modT[0][:, 8 + kc:9 + kc])
                        return last

                    def ev_dve():
                        last = None
                        for kc in (4, 5, 6, 7):
                            last = dve.tensor_scalar(out=hT[:, kc, :, tt * 8:(tt + 1) * 8],
                                                     in0=pv[:, kc * 128:(kc + 1) * 128].rearrange("p (j r) -> p r j", r=16),
                                                     scalar1=modT[0][:, 8 + kc:9 + kc], scalar2=modT[0][:, kc:kc + 1],
                                                     op0=ALU.mult, op1=ALU.add)
                        return last
                    op("act", ev_act, reads=B(b0) + ["modT0"], writes=["hTa%d" % tt])
                    op("dve", ev_dve, reads=B(b0 + 1) + ["modT0"], writes=["hTd%d" % tt])
                S.barrier()

            with contextlib.ExitStack() as ml:
                Wg = [sbuf(ml, "Wg%d" % s, [128, 8, 3, 128], BF16) for s in range(2)]
                Wgt = sbuf(ml, "Wgt", [128, 8, 128], BF16)
                QT = sbuf(ml, "QT", [128, TOWN], BF16)
                KT = sbuf(ml, "KT", [128, 6144], BF16)
                Vt = sbuf(ml, "Vt", [128, 48, 128], BF16)
                sg = sbuf(ml, "sg", [128, TOWN], BF16)
                acc = sbuf(ml, "acc", [128, 2, TOWN], F32)
                Ssb = [[sbuf(ml, "Ssb%d%d" % (a, s), [128, 512], F32) for s in range(2)] for a in range(2)]
                PT = [[sbuf(ml, "PT%d%d" % (a, s), [128, 512], BF16) for s in range(2)] for a in range(2)]
                yst = sbuf(ml, "yst", [128, TOWN], BF16)
                vst = [sbuf(ml, "vst%d" % s, [128, 512], BF16) for s in range(2)]
                vsl = [0]

                op("pool", lambda: pool.memset(KT[:], 0.0), writes=["KTall"])
                op("pool", lambda: pool.memset(Vt[:], 0.0), writes=["Vtall"])
                op("pool", lambda: pool.memset(vst[0][:], 0.0), writes=["vst0"])
                op("pool", lambda: pool.memset(vst[1][:], 0.0), writes=["vst1"])
                S.barrier()

                awv = awin_d.rearrange("(kc p) n -> p kc n", p=128)
                its = [(hp, g) for hp in range(8) for g in range(3)]

                def load_W(it):
                    hp, g = its[it]
                    s = it % 2

                    def f():
                        r = []
                        for wh in range(3):
                            c0 = g * 3072 + wh * 1024 + hp * 128
                            r.append(pool.dma_start(out=Wg[s][:, :, wh, :], in_=awv[:, :, c0:c0 + 128]))
                        return r
                    op("pool", f, writes=["Wg%d" % s], dma="Wg%d" % s)

                def load_Wgt(hp):
                    c0 = 9216 + hp * 128
                    op("pool", lambda: pool.dma_start(out=Wgt[:], in_=awv[:, :, c0:c0 + 128]), writes=["Wgt"], dma="Wgt")

                ipb = [0]

                def next_bank():
                    b = 6 + ipb[0] % 2
                    ipb[0] += 1
                    return b

                evq = [0]

                def evac_copy(out_ap, in_ap, reads, writes):
                    evq[0] += 1
                    if evq[0] % 3 != 0:
                        op("act", lambda: act.activation(out=out_ap, in_=in_ap, func=AF.Identity), reads=reads, writes=writes)
                    else:
                        op("dve", lambda: dve.tensor_copy(out=out_ap, in_=in_ap), reads=reads, writes=writes)

                norm_q = []
                norm_done = [8]

                def norm_upto(c):
                    while norm_q and norm_done[0] <= c:
                        norm_q.pop(0)()
                        norm_done[0] += 1

                load_W(0)
                load_Wgt(0)
                if not stop_after_l0:
                    steps1, fin1 = mod_steps(1, ml, 2, banks=(6, 7, 6))
                else:
                    steps1 = []

                for it, (hp, g) in enumerate(its):
                    if steps1:
                        steps1.pop(0)()
                    d = GROUP_D[g]
                    Lq = TOWN // d
                    Lk = Lq + 64
                    Lkp = Lk + 64
                    ntr = Lkp // 128
                    ws = it % 2
                    W = Wg[ws]
                    wname = "Wg%d" % ws
                    if it + 1 < len(its):
                        load_W(it + 1)

                    km = 16 // d

                    def hrhs(kc, r, j0, n):
                        v = hT[:, kc, :, :].rearrange("p (m r) q -> p r m q", r=d)
                        return v[:, r, :, j0 // km:(j0 + n) // km]

                    def psmq(bk, o, n):
                        return ps[:, bk, o:o + n].rearrange("p (m q) -> p m q", m=km)

                    def pqm(bk, o, n):
                        return ps[:, bk, o:o + n].rearrange("p (m q) -> p q m", m=km)

                    def unperm(ap2d):
                        return ap2d.rearrange("p (q m) -> p q m", m=km)

                    def proj(wh, bk, r, j0, n, o=0):
                        def mm():
                            last = None
                            for kc in range(8):
                                last = pe.matmul(psmq(bk, o, n), lhsT=W[:, kc, wh, :], rhs=hrhs(kc, r, j0, n),
                                                 start=(kc == 0), stop=(kc == 7))
                            return last
                        op("pe", mm, reads=[wname], writes=B(bk))

                    def q_stage(r, i):
                        if Lq >= 512:
                            if i * 512 >= Lq:
                                return
                            bk = next_bank()
                            pi0 = r * Lq + i * 512
                            proj(0, bk, r, i * 512, 512)
                            evac_copy(unperm(QT[:, pi0:pi0 + 512]), pqm(bk, 0, 512), B(bk), ["QT%d" % (pi0 // 512)])
                        else:
                            if i != 0 or r % 2 != 0:
                                return
                            bk = next_bank()
                            pi0 = r * Lq

                            def mm():
                                last = None
                                for kc in range(8):
                                    last = pe.matmul(ps[:, bk, :].rearrange("p (a b) -> p a b", a=2), lhsT=W[:, kc, 0, :],
                                                     rhs=hT[:, kc, r:r + 2, 0:Lq], start=(kc == 0), stop=(kc == 7))
                                return last
                            op("pe", mm, reads=[wname], writes=B(bk))
                            evac_copy(QT[:, pi0:pi0 + 512], ps[:, bk, :], B(bk), ["QT%d" % (pi0 // 512)])

                    def k_stage(r, i):
                        j0 = i * 512
                        if j0 >= Lk:
                            return
                        n = min(512, Lk - j0)
                        bk = next_bank()
                        proj(1, bk, r, j0, n)
                        o0 = r * Lkp + 64 + j0
                        evac_copy(unperm(KT[:, o0:o0 + n]), pqm(bk, 0, n), B(bk), ["KT_%d_%d" % (r, i)])

                    def v_stage_mm(r, i):
                        c0 = 4 * i
                        if c0 >= ntr:
                            return None
                        nt = min(4, ntr - c0)
                        off = 64 if c0 == 0 else 0
                        jst = 128 * c0 - 64 + off
                        n = 128 * nt - off
                        bk = next_bank()
                        vs = vsl[0] % 2
                        vsl[0] += 1
                        proj(2, bk, r, jst, n, o=off)
                        evac_copy(unperm(vst[vs][:, off:off + n]), pqm(bk, off, n), B(bk), ["vst%d" % vs])
                        return (c0, nt, vs)

                    def v_stage_tr(r, st):
                        if st is None:
                            return
                        c0, nt, vs = st
                        bk2 = next_bank()
                        ptb = ps[:, bk2, :].bitcast(BF16)

                        def tr():
                            last = None
                            for ti in range(nt):
                                last = pe.transpose(ptb[:, ti * 128:(ti + 1) * 128], vst[vs][:, ti * 128:(ti + 1) * 128], ident_b[:])
                            return last
                        op("pe", tr, reads=["vst%d" % vs, "ident_b"], writes=B(bk2))
                        t0 = r * ntr + c0
                        evac_copy(Vt[:, t0:t0 + nt, :], ptb[:, 0:nt * 128].rearrange("p (a b) -> p a b", a=nt), B(bk2),
                                  ["V%d" % (t0 + ti) for ti in range(nt)])

                    def gate_stage(c):
                        norm_upto(c)
                        bk = next_bank()

                        def mm():
                            last = None
                            for kc in range(8):
                                last = pe.matmul(ps[:, bk, :].rearrange("p (m q) -> p m q", m=16), lhsT=Wgt[:, kc, :],
                                                 rhs=hT[:, kc, :, c * 32:(c + 1) * 32], start=(kc == 0), stop=(kc == 7))
                            return last
                        op("pe", mm, reads=["Wgt"], writes=B(bk))
                        op("act", lambda: act.activation(out=sg[:, c * 512:(c + 1) * 512].rearrange("p (q m) -> p q m", m=16),
                                                         in_=ps[:, bk, :].rearrange("p (m q) -> p q m", m=16), func=AF.Silu),
                           reads=B(bk), writes=["sg%d" % c])

                    cs = [8.0 * SLOPES[g][2 * hp + a] * d for a in range(2)]

                    def kt_chunks(r, pb):
                        lo = max(0, 256 * pb - 64) // 512
                        hi = min(Lk - 1, 256 * pb + 319) // 512
                        return ["KT_%d_%d" % (r, i) for i in range(lo, hi + 1)]

                    def emit_qk(blk):
                        r, pb, bi = blk
                        sl = bi % 2
                        qname = "QT%d" % ((r * Lq + 256 * pb) // 512)
                        for a in range(2):
                            bk = 2 * sl + a

                            def mm():
                                last = None
                                for qb2 in range(2):
                                    b = 2 * pb + qb2
                                    for kt in range(2):
                                        c = b + kt
                                        col = (qb2 * 2 + kt) * 128
                                        last = pe.matmul(ps[:, bk, col:col + 128],
                                                         lhsT=KT[64 * a:64 * a + 64, r * Lkp + 128 * c: r * Lkp + 128 * c + 128],
                                                         rhs=QT[64 * a:64 * a + 64, r * Lq + 128 * b: r * Lq + 128 * b + 128],
                                                         start=True, stop=True)
                                return last
                            op("pe", mm, reads=[qname] + kt_chunks(r, pb), writes=B(bk))

                    def emit_sm(blk):
                        r, pb, bi = blk
                        sl = bi % 2
                        bt = base_first if pb == 0 else base_reg
                        for a in range(2):
                            bk = 2 * sl + a
                            op("dve", lambda: dve.scalar_tensor_tensor(out=Ssb[a][sl][:], in0=bt[:], scalar=cs[a], in1=ps[:, bk, :],
                                                                       op0=ALU.mult, op1=ALU.add),
                               reads=B(bk), writes=["Ssb%d%d" % (a, sl)])
                            op("act", lambda: act.activation(out=PT[a][sl][:], in_=Ssb[a][sl][:], func=AF.Exp, scale=0.125),
                               reads=["Ssb%d%d" % (a, sl)], writes=["PT%d%d" % (a, sl)])

                    def emit_pv(blk):
                        r, pb, bi = blk
                        sl = bi % 2
                        bk = 4 + (bi % 2)
                        vnames = ["V%d" % (r * ntr + 2 * pb + k) for k in range(3)]

                        for a in range(2):
                            def mm():
                                last = None
                                for qb2 in range(2):
                                    b = 2 * pb + qb2
                                    for nd in range(2):
                                        for kt in range(2):
                                            tau = r * ntr + b + kt
                                            col = (qb2 * 2 + kt) * 128
                                            lh = Vt[:, tau, 64 * a:64 * a + 64] if nd == 0 else ones_bf[:, 0:64]
                                            oc = nd * 256 + qb2 * 128
                                            last = pe.matmul(ps[64 * a:64 * a + 64, bk, oc:oc + 128], lhsT=lh,
                                                             rhs=PT[a][sl][:, col:col + 128], start=(kt == 0), stop=(kt == 1),
                                                             tile_position=(0, 64 * a))
                                return last
                            op("pe", mm, reads=["PT%d%d" % (a, sl), "ones_bf"] + vnames, writes=B(bk))
                        accv = acc[:].rearrange("p n (j r) -> p n r j", r=d)[:, :, r, 256 * pb:256 * pb + 256]
                        odv = ps[:, bk, :].rearrange("p (n q) -> p n q", n=2)
                        if g == 0:
                            op("dve", lambda: dve.tensor_copy(out=accv, in_=odv), reads=B(bk), writes=["acc"])
                        else:
                            op("dve", lambda: dve.tensor_tensor(out=accv, in0=accv, in1=odv, op=ALU.add),
                               reads=B(bk) + ["acc"], writes=["acc"])

                    nst = max((Lk + 511) // 512, (ntr + 3) // 4)
                    G = []
                    for r in range(d):
                        for i in range(nst):
                            vbox = [None]

                            def g_vmm(r=r, i=i, vbox=vbox):
                                vbox[0] = v_stage_mm(r, i)

                            def g_k(r=r, i=i):
                                k_stage(r, i)

                            def g_vtr(r=r, vbox=vbox):
                                v_stage_tr(r, vbox[0])

                            def g_q(r=r, i=i):
                                q_stage(r, i)
                            for fn in (g_vmm, g_k, g_vtr, g_q):
                                G.append(((r, i), fn))
                    gates = list(range(TOWN // 512)) if g == 0 else []
                    gi = [0]

                    def emit_group():
                        key, fn = G[gi[0]]
                        gi[0] += 1
                        fn()
                        if gates and gi[0] % max(1, len(G) // 8) == 0:
                            gate_stage(gates.pop(0))

                    pending = None
                    bi = 0
                    for r in range(d):
                        for pb in range(Lq // 256):
                            req = (r, (256 * pb + 383) // 512)
                            while gi[0] < len(G) and G[gi[0]][0] <= req:
                                emit_group()
                            blk = (r, pb, bi)
                            bi += 1
                            if g == 0:
                                norm_upto(pb // 2)
                            else:
                                norm_upto(7)
                            emit_qk(blk)
                            emit_sm(blk)
                            if gi[0] < len(G):
                                emit_group()
                            if pending is not None:
                                emit_pv(pending)
                            pending = blk
                    while gi[0] < len(G):
                        emit_group()
                    if pending is not None:
                        emit_pv(pending)
                    if g == 0:
                        while gates:
                            gate_stage(gates.pop(0))
                        if hp + 1 < 8:
                            load_Wgt(hp + 1)

                    if g == 2:
                        def mk_piece(c, hp=hp):
                            def piece():
                                sl_ = slice(c * 512, (c + 1) * 512)
                                op("dve", lambda: dve.reciprocal(out=acc[:, 1, sl_], in_=acc[:, 1, sl_]), reads=["acc"], writes=["acc"])
                                op("dve", lambda: dve.tensor_tensor(out=acc[:, 0, sl_], in0=acc[:, 0, sl_], in1=acc[:, 1, sl_],
                                                                    op=ALU.mult), reads=["acc"], writes=["acc"])
                                op("dve", lambda: dve.tensor_tensor(out=yst[:, sl_], in0=acc[:, 0, sl_], in1=sg[:, sl_], op=ALU.mult),
                                   reads=["acc", "sg%d" % c], writes=["yst"])
                                if c == 7:
                                    op("sp", lambda: sp.dma_start(out=yts_d[:, hp, :], in_=yst[:]), reads=["yst"], dma="yst")
                            return piece
                        norm_q.extend(mk_piece(c) for c in range(8))
                        norm_done[0] = 0
                norm_upto(7)
                S.barrier()
        S.barrier()

        if not stop_after_l0:
            Wib = sbuf(es, "Wib", [128, 8, 6144], BF16)
            Wob = sbuf(es, "Wob", [128, 16, D], BF16)
            wsT = sbuf(es, "wsT", [128, 16, 128], BF16)
        with contextlib.ExitStack() as lo:
            WoA = sbuf(lo, "WoA", [128, 8, D], BF16)
            op("pool", lambda: pool.dma_start(out=WoA[:], in_=awout_d.rearrange("(kc p) n -> p kc n", p=128)),
               writes=["WoA"], dma="WoA")
            phase_params(0, lo)
            for kc in range(8):
                op("dve" if kc % 2 == 0 else "pool",
                   (lambda kc=kc: (dve if kc % 2 == 0 else pool).tensor_tensor(out=WoA[:, kc, :], in0=WoA[:, kc, :],
                                                                                in1=PH["gate_bc"][:], op=ALU.mult)),
                   reads=["WoA", "gate_bc"], writes=["WoA_g%d" % kc])
            S.barrier()
            if not stop_after_l0:
                bwv = bwin_d.rearrange("(kc p) n -> p kc n", p=128)
                for kc in range(8):
                    op("pool", lambda: pool.dma_start(out=Wib[:, kc, :], in_=bwv[:, kc, :]), writes=["Wib%d" % kc], dma="Wib%d" % kc)
                bov = bwout_d.rearrange("(kc p) n -> p kc n", p=128)
                for q in range(4):
                    op("pool", lambda: pool.dma_start(out=Wob[:, 4 * q:4 * q + 4, :], in_=bov[:, 4 * q:4 * q + 4, :]),
                       writes=["Wob%d" % q], dma="Wob%d" % q)
                op("pool", lambda: pool.dma_start(out=wsT[:], in_=wsT_d[:, :, :]), writes=["wsT"], dma="wsT")
            yin = [sbuf(lo, "yin%d" % s, [128, 8, 512], BF16) for s in range(2)]
            xr = [sbuf(lo, "xr%d" % s, [128, D], F32) for s in range(2)]
            ot = [sbuf(lo, "ot%d" % s, [128, D], F32) for s in range(4)]
            def lo_loads(tt):
                q4, ys = tt // 4, (tt // 4) % 2
                if tt % 4 == 0:
                    op("sp", lambda: sp.dma_start(out=yin[ys][:], in_=yts_d[:, :, q4 * 512:(q4 + 1) * 512]),
                       writes=["yin%d" % ys], dma="yin%d" % ys)
                s = tt % 2
                op("sp", lambda: sp.dma_start(out=xr[s][:], in_=x_d[tt * 128:(tt + 1) * 128, :]),
                   writes=["xr%d" % s], dma="xr%d" % s)
            STs = []
            for q in range(2):
                STs.append({"stat": sbuf(lo, "lstat%d" % q, [128, 2, 6], F32), "mv": sbuf(lo, "lmv%d" % q, [128, 2], F32),
                            "rstd": sbuf(lo, "lrstd%d" % q, [128, 1], F32), "nmr": sbuf(lo, "lnmr%d" % q, [128, 1], F32),
                            "sfx": "_l%d" % q})
            NT = TOWN // 128

            def lo_E1(tt):
                q4, ys = tt // 4, (tt // 4) % 2
                s = tt % 2
                b0 = 2 * (tt % 4)

                def mm():
                    last = None
                    for hf in range(2):
                        for kc in range(8):
                            last = pe.matmul(ps[:, b0 + hf, :], lhsT=yin[ys][:, kc, (tt % 4) * 128:(tt % 4 + 1) * 128],
                                             rhs=WoA[:, kc, hf * 512:(hf + 1) * 512], start=(kc == 0), stop=(kc == 7))
                    return last
                op("pe", mm, reads=["yin%d" % ys, "WoA"], writes=B(b0, b0 + 1))
                epi_E1(b0, xr[s][:], "xr%d" % s, ot[tt % 4], "ot%d" % (tt % 4), STs[s])

            def lo_E2(tt):
                s = tt % 2
                epi_E2(ot[tt % 4], "ot%d" % (tt % 4), x1s_d[tt * 128:(tt + 1) * 128, :], "ot%d" % (tt % 4), STs[s], gb="dve")

            lo_loads(0)
            lo_loads(1)
            lo_E1(0)
            for tt in range(NT):
                if tt + 1 < NT:
                    lo_E1(tt + 1)
                lo_E2(tt)
                if tt + 2 < NT:
                    lo_loads(tt + 2)
            S.barrier()
        S.barrier()

        if not stop_after_l0:
            with contextlib.ExitStack() as l1:
                phase_params(1, l1)
                for kc in range(16):
                    op("dve" if kc % 2 == 0 else "pool",
                       (lambda kc=kc: (dve if kc % 2 == 0 else pool).tensor_tensor(out=Wob[:, kc, :], in0=Wob[:, kc, :],
                                                                                    in1=PH["gate_bc"][:], op=ALU.mult)),
                       reads=["gate_bc"], writes=["Wob_g%d" % kc])
                Cb = sbuf(l1, "Cb", [128, 16, 128], F32)
                lng_bc = sbuf(l1, "lng_bc", [128, 2048], F32)
                op("sp", lambda: sp.dma_start(out=lng_bc[:], in_=blg_d[0:1, :].partition_broadcast(128)),
                   writes=["lng_bc"], dma="lng")
                with contextlib.ExitStack() as s1:
                    lnb_t = sbuf(s1, "lnb_t", [128, 16, 128], F32)
                    wsN = sbuf(s1, "wsN", [128, 16, 128], F32)
                    bsT = sbuf(s1, "bsT", [128, 16], F32)
                    rsw = sbuf(s1, "rsw", [128, 16], F32)
                    op("sp", lambda: sp.dma_start(out=lnb_t[:].rearrange("p a b -> p (a b)"),
                                                  in_=blb_d[0:1, :].partition_broadcast(128)), writes=["lnb_t"], dma="lnb")
                    op("sp", lambda: sp.dma_start(out=wsN[:], in_=wsN_d[:, :, :]), writes=["wsN"], dma="wsN")
                    op("sp", lambda: sp.dma_start(out=bsT[:], in_=bsT_d[:, :]), writes=["bsT"], dma="bsT")
                    op("dve", lambda: dve.reduce_sum(out=rsw[:], in_=wsN[:], axis=mybir.AxisListType.X),
                       reads=["wsN"], writes=["rsw"])
                    op("dve", lambda: dve.tensor_tensor(out=Cb[:], in0=lnb_t[:], in1=rsw[:].unsqueeze(2).to_broadcast([128, 16, 128]),
                                                        op=ALU.mult), reads=["lnb_t", "rsw"], writes=["Cb"])
                    op("dve", lambda: dve.tensor_tensor(out=Cb[:], in0=Cb[:], in1=bsT[:].unsqueeze(2).to_broadcast([128, 16, 128]),
                                                        op=ALU.add), reads=["Cb", "bsT"], writes=["Cb"])
                    S.barrier()
                Cflat = Cb[:].rearrange("p a b -> p (a b)")

                x1t = [sbuf(l1, "x1t%d" % s, [128, D], F32) for s in range(2)]
                h1Ts = [sbuf(l1, "h1T%d" % q, [128, 8, 128], BF16) for q in range(2)]
                vf = sbuf(l1, "vf", [128, 2048], F32)
                vhat = sbuf(l1, "vhat", [128, 2048], BF16)
                ubf = sbuf(l1, "ubf", [128, 2048], BF16)
                sgb = sbuf(l1, "sgb", [128, 2048], BF16)
                yT = sbuf(l1, "yT", [128, 16, 128], BF16)
                o1s = [sbuf(l1, "o1_%d" % q, [128, D], F32) for q in range(2)]
                wib_all = ["Wib%d" % kc for kc in range(8)]
                wob_all = ["Wob%d" % q for q in range(4)]
                NCH = TOWN // 128

                stat2 = sbuf(l1, "stat2", [128, 4, 6], F32)
                mv2 = sbuf(l1, "mv2", [128, 2], F32)
                rstd2 = sbuf(l1, "rstd2", [128, 1], F32)
                nmr2 = sbuf(l1, "nmr2", [128, 1], F32)
                vfn = ["vf%d" % j for j in range(4)]
                ubn = ["ubf%d" % j for j in range(4)]
                sgn1 = ["sgb%d" % j for j in range(4)]
                rot = [0]

                def load_x1(tt):
                    s = tt % 2
                    op("sp", lambda: sp.dma_start(out=x1t[s][:], in_=x1s_d[tt * 128:(tt + 1) * 128, :]),
                       writes=["x1t%d" % s], dma="x1t%d" % s)

                def stage_A(tt):
                    s = tt % 2
                    h1T = h1Ts[s]
                    pv = psflat[:, 0:1024]

                    def tr():
                        last = None
                        for kc in range(8):
                            last = pe.transpose(pv[:, kc * 128:(kc + 1) * 128], x1t[s][:, kc * 128:(kc + 1) * 128], ident_f[:])
                        return last
                    op("pe", tr, reads=["x1t%d" % s, "ident_f"], writes=B(0, 1))

                    def ev_act():
                        last = None
                        for kc in range(8):
                            last = act.activation(out=h1T[:, kc, :], in_=pv[:, kc * 128:(kc + 1) * 128], func=AF.Identity,
                                                  bias=modT[1][:, kc:kc + 1], scale=modT[1][:, 8 + kc:9 + kc])
                        return last
                    op("act", ev_act, reads=B(0, 1) + ["modT1"], writes=["h1T%d" % s])

                def inproj_chunk(tt, cc):
                    bk = rot[0] % 4
                    rot[0] += 1
                    h1T = h1Ts[tt % 2]

                    def mm():
                        last = None
                        for kc in range(8):
                            last = pe.matmul(ps[:, bk, :], lhsT=h1T[:, kc, :], rhs=Wib[:, kc, cc * 512:(cc + 1) * 512],
                                             start=(kc == 0), stop=(kc == 7))
                        return last
                    op("pe", mm, reads=["h1T%d" % (tt % 2)] + wib_all, writes=B(bk))
                    return bk

                def stage_B1a(tt):
                    for j in range(4):
                        bk = inproj_chunk(tt, 4 + j)
                        op("act", lambda: act.activation(out=vf[:, j * 512:(j + 1) * 512], in_=ps[:, bk, :], func=AF.Gelu),
                           reads=B(bk), writes=["vf%d" % j])

                def stage_B1b(tt):
                    for j in range(4):
                        op("dve", lambda: dve.bn_stats(out=stat2[:, j, :], in_=vf[:, j * 512:(j + 1) * 512]),
                           reads=["vf%d" % j], writes=["stat2_%d" % j])
                    op("dve", lambda: dve.bn_aggr(out=mv2[:], in_=stat2[:].rearrange("p a b -> p (a b)")),
                       reads=["stat2_%d" % j for j in range(4)], writes=["mv2"])
                    op("act", lambda: act.activation(out=rstd2[:], in_=mv2[:, 1:2], func=AF.Sqrt, bias=eps_t[:], scale=1.0),
                       reads=["mv2", "eps_t"], writes=["rstd2"])
                    op("dve", lambda: dve.reciprocal(out=rstd2[:], in_=rstd2[:]), reads=["rstd2"], writes=["rstd2"])
                    op("dve", lambda: dve.scalar_tensor_tensor(out=nmr2[:], in0=mv2[:, 0:1], scalar=-1.0, in1=rstd2[:],
                                                               op0=ALU.mult, op1=ALU.mult),
                       reads=["mv2", "rstd2"], writes=["nmr2"])
                    op("act", lambda: act.activation(out=vhat[:], in_=vf[:], func=AF.Identity, bias=nmr2[:], scale=rstd2[:]),
                       reads=vfn + ["rstd2", "nmr2"], writes=["vhat"])

                def stage_B2(tt, js):
                    for j8 in js:
                        if j8 < 4:
                            j = j8
                            bk = inproj_chunk(tt, 8 + j)
                            op("act", lambda: act.activation(out=sgb[:, j * 512:(j + 1) * 512], in_=ps[:, bk, :], func=AF.Silu),
                               reads=B(bk), writes=["sgb%d" % j])
                        else:
                            j = j8 - 4
                            bk = inproj_chunk(tt, j)
                            op("act", lambda: act.activation(out=ubf[:, j * 512:(j + 1) * 512], in_=ps[:, bk, :], func=AF.Gelu),
                               reads=B(bk), writes=["ubf%d" % j])

                def stage_C(tt):
                    def sp_mm():
                        last = None
                        for gi in range(16):
                            last = pe.matmul(ps[:, 4 + gi // 4, (gi % 4) * 128:(gi % 4 + 1) * 128], lhsT=wsT[:, gi, :],
                                             rhs=vhat[:, gi * 128:(gi + 1) * 128], start=True, stop=True)
                        return last
                    op("pe", sp_mm, reads=["vhat", "wsT"], writes=B(4, 5, 6, 7))
                    svp = psflat[:, 4 * 512:8 * 512]
                    op("dve", lambda: dve.tensor_tensor(out=vf[:], in0=svp, in1=lng_bc[:], op=ALU.mult),
                       reads=B(4, 5, 6, 7) + ["lng_bc"] + vfn, writes=vfn)
                    op("dve", lambda: dve.tensor_tensor(out=vf[:], in0=vf[:], in1=Cflat, op=ALU.add), reads=vfn + ["Cb"], writes=vfn)
                    op("dve", lambda: dve.tensor_tensor(out=vf[:], in0=vf[:], in1=ubf[:], op=ALU.mult),
                       reads=vfn + ubn, writes=vfn)
                    op("dve", lambda: dve.tensor_tensor(out=ubf[:], in0=vf[:], in1=sgb[:], op=ALU.mult),
                       reads=vfn + sgn1 + ubn, writes=ubn)

                def stage_D(tt):
                    ytp = psflat[:, 4 * 512:6 * 512].bitcast(BF16)

                    def ytr():
                        last = None
                        for gi in range(16):
                            last = pe.transpose(ytp[:, gi * 128:(gi + 1) * 128], ubf[:, gi * 128:(gi + 1) * 128], ident_b[:])
                        return last
                    op("pe", ytr, reads=ubn + ["ident_b"], writes=B(4, 5))
                    op("dve", lambda: dve.tensor_copy(out=yT[:].rearrange("p a b -> p (a b)"), in_=ytp),
                       reads=B(4, 5), writes=["yT"])

                def stage_E(tt):
                    s = tt % 2

                    def omm():
                        last = None
                        for hf in range(2):
                            for kc in range(16):
                                last = pe.matmul(ps[:, 6 + hf, :], lhsT=yT[:, kc, :], rhs=Wob[:, kc, hf * 512:(hf + 1) * 512],
                                                 start=(kc == 0), stop=(kc == 15))
                        return last
                    op("pe", omm, reads=["yT"] + wob_all, writes=B(6, 7))
                    epilogue(6, x1t[s][:], "x1t%d" % s, o1s[s], "o1_%d" % s, out_d[tt * 128:(tt + 1) * 128, :], "o1_%d" % s)

                load_x1(0)
                load_x1(1)
                stage_A(0)
                stage_B1a(0)
                stage_B1b(0)
                stage_B2(0, range(8))
                stage_A(1)
                for tt in range(NCH):
                    nx = tt + 1 < NCH
                    stage_C(tt)
                    if nx:
                        stage_B1a(tt + 1)
                    stage_D(tt)
                    if nx:
                        stage_B1b(tt + 1)
                        stage_B2(tt + 1, range(0, 2))
                    stage_E(tt)
                    if tt + 2 < NCH:
                        load_x1(tt + 2)
                    if nx:
                        stage_B2(tt + 1, range(2, 8))
                    if tt + 2 < NCH:
                        stage_A(tt + 2)
                S.barrier()
        S.barrier()
    return nc


def _prep_inputs(inputs):
    x = np.asarray(inputs["x"], dtype=np.float32)
    c = np.asarray(inputs["c"], dtype=np.float32)
    w_s = np.asarray(inputs["b_w_s"], dtype=np.float32)[0]
    b_s = np.asarray(inputs["b_b_s"], dtype=np.float32)[0]
    shared = {
        "ada_w": np.ascontiguousarray(inputs["ada_w"], dtype=np.float32),
        "ada_b": np.ascontiguousarray(inputs["ada_b"], dtype=np.float32),
        "post_ln_g": np.ascontiguousarray(inputs["post_ln_g"], dtype=np.float32),
        "post_ln_b": np.ascontiguousarray(inputs["post_ln_b"], dtype=np.float32),
        "a_w_in": np.ascontiguousarray(np.asarray(inputs["a_w_in"], dtype=np.float32)[0]),
        "a_w_out": np.ascontiguousarray(np.asarray(inputs["a_w_out"], dtype=np.float32)[0]),
        "b_w_in": np.ascontiguousarray(np.asarray(inputs["b_w_in"], dtype=np.float32)[0]),
        "b_ln_g": np.ascontiguousarray(np.asarray(inputs["b_ln_g"], dtype=np.float32)[0:1]),
        "b_ln_b": np.ascontiguousarray(np.asarray(inputs["b_ln_b"], dtype=np.float32)[0:1]),
        "b_w_out": np.ascontiguousarray(np.asarray(inputs["b_w_out"], dtype=np.float32)[0]),
    }
    in_maps = []
    for core in range(8):
        b, half = core // 2, core % 2
        if half == 0:
            xl = x[b, 0:TLOC]
            ws, bs = w_s, b_s
        else:
            xl = x[b, ::-1][0:TLOC]
            ws, bs = w_s[:, ::-1, ::-1], b_s[:, ::-1]
        m = dict(shared)
        m["x"] = np.ascontiguousarray(xl)
        m["ct"] = np.ascontiguousarray(c[b].reshape(8, 128).T)
        m["w_sT"] = np.ascontiguousarray(ws.transpose(2, 0, 1))
        m["w_sN"] = np.ascontiguousarray(ws.transpose(1, 0, 2))
        m["b_sT"] = np.ascontiguousarray(bs.T)
        in_maps.append(m)
    return in_maps


def kernel(**inputs):
    in_maps = _prep_inputs(inputs)
    nc = build()
    res = run_bass_kernel_spmd(nc, in_maps, core_ids=list(range(8)))
    out = np.empty((4, SEQ, D), dtype=np.float32)
    for core in range(8):
        b, half = core // 2, core % 2
        o = np.asarray(res.results[core]["out"], dtype=np.float32)
        if half == 0:
            out[b, 0:TOWN] = o
        else:
            out[b, TOWN:SEQ] = o[::-1]
    return out
```

```python
import contextlib
import numpy as np
import concourse.bass as bass
import concourse.mybir as mybir
from concourse.bass_utils import run_bass_kernel_spmd

F32 = mybir.dt.float32
BF16 = mybir.dt.bfloat16
AF = mybir.ActivationFunctionType
ALU = mybir.AluOpType

D = 1024
SEQ = 8192
TOWN = 4096
TLOC = 5120
NEGBIG = -1.0e6
ALPHA = float((2 * 2) ** 0.25)
EPS = 1e-5
GROUP_D = (1, 4, 16)
SLOPES = [[2.0 ** (-8.0 * (g * 16 + h + 1) / 48.0) for h in range(16)] for g in range(3)]


class Sched:
    def __init__(self, nc, es):
        self.nc = nc
        self.es = es
        self.eng = {"pe": nc.tensor, "act": nc.scalar, "dve": nc.vector, "pool": nc.gpsimd, "sp": nc.sync}
        self.sems = {}
        self.count = {}
        for k in ("pe", "act", "dve", "pool"):
            self.sems[k] = es.enter_context(nc.semaphore("sem_" + k))
            self.count[k] = 0
        self.waited = {k: {} for k in self.eng}
        self.last_w = {}
        self.readers = {}

    def _dsem(self, key):
        if key not in self.sems:
            self.sems[key] = self.es.enter_context(self.nc.semaphore("dsem_" + key))
            self.count[key] = 0
        return self.sems[key]

    def _wait(self, eng, ev):
        key, val = ev
        if self.waited[eng].get(key, 0) >= val:
            return
        self.eng[eng].wait_ge(self.sems[key], val)
        self.waited[eng][key] = val

    def op(self, eng, fn, reads=(), writes=(), dma=None):
        deps = set()
        for r in reads:
            ev = self.last_w.get(r)
            if ev is not None:
                deps.add(ev)
        for w in writes:
            ev = self.last_w.get(w)
            if ev is not None:
                deps.add(ev)
            for ev in self.readers.get(w, ()):
                deps.add(ev)
        for ev in deps:
            if eng == "pe" and ev[0] == "pe":
                continue
            self._wait(eng, ev)
        res = fn()
        if dma is not None:
            sem = self._dsem(dma)
            insts = res if isinstance(res, (list, tuple)) else [res]
            for ins in insts:
                ins.then_inc(sem, 16)
            self.count[dma] += 16 * len(insts)
            ev = (dma, self.count[dma])
        else:
            res.then_inc(self.sems[eng], 1)
            self.count[eng] += 1
            ev = (eng, self.count[eng])
        for w in writes:
            self.last_w[w] = ev
            self.readers[w] = []
        for r in reads:
            self.readers.setdefault(r, []).append(ev)
        return ev

    def barrier(self):
        for e in self.eng:
            for key, cnt in self.count.items():
                if cnt > 0:
                    self._wait(e, (key, cnt))
        self.last_w = {}
        self.readers = {}


def build(stop_after_l0=False):
    nc = bass.Bass("TRN2", target_bir_lowering=False)

    def din(name, shape, dt=F32):
        return nc.dram_tensor(name, shape, dt, kind="ExternalInput").ap()

    x_d = din("x", [TLOC, D])
    ct_d = din("ct", [128, 8])
    adaw_d = din("ada_w", [2, D, 3 * D])
    adab_d = din("ada_b", [2, 3 * D])
    plg_d = din("post_ln_g", [2, D])
    plb_d = din("post_ln_b", [2, D])
    awin_d = din("a_w_in", [D, 10240])
    awout_d = din("a_w_out", [D, D])
    bwin_d = din("b_w_in", [D, 6144])
    blg_d = din("b_ln_g", [1, 2048])
    blb_d = din("b_ln_b", [1, 2048])
    wsT_d = din("w_sT", [128, 16, 128])
    wsN_d = din("w_sN", [128, 16, 128])
    bsT_d = din("b_sT", [128, 16])
    bwout_d = din("b_w_out", [2048, D])
    yts_d = nc.dram_tensor("yts", [128, 8, TOWN], BF16, kind="Internal").ap()
    if stop_after_l0:
        x1s_d = nc.dram_tensor("x1s", [TOWN, D], F32, kind="ExternalOutput").ap()
        out_d = None
    else:
        x1s_d = nc.dram_tensor("x1s", [TOWN, D], F32, kind="Internal").ap()
        out_d = nc.dram_tensor("out", [TOWN, D], F32, kind="ExternalOutput").ap()

    with contextlib.ExitStack() as es:
        S = Sched(nc, es)
        op = S.op
        pe, act, dve, pool, sp = nc.tensor, nc.scalar, nc.vector, nc.gpsimd, nc.sync

        def sbuf(stack, name, shape, dt):
            return stack.enter_context(nc.sbuf_tensor(name, shape, dt))

        ps = es.enter_context(nc.psum_tensor("ps", [128, 8, 512], F32))
        psflat = ps[:].rearrange("p b n -> p (b n)")

        def bank(b):
            return ps[:, b, :]

        def B(*bs):
            return ["ps%d" % b for b in bs]

        ident_f = sbuf(es, "ident_f", [128, 128], F32)
        ident_b = sbuf(es, "ident_b", [128, 128], BF16)
        ones_bf = sbuf(es, "ones_bf", [128, 64], BF16)
        ones_row = sbuf(es, "ones_row", [1, 128], F32)
        cond = sbuf(es, "cond", [128, 8], F32)
        modT = [sbuf(es, "modT%d" % i, [128, 24], F32) for i in range(2)]
        PH = {}
        stat = sbuf(es, "stat", [128, 4, 6], F32)
        mv = sbuf(es, "mv", [128, 2], F32)
        rstd = sbuf(es, "rstd", [128, 1], F32)
        nmr = sbuf(es, "nmr", [128, 1], F32)
        eps_t = sbuf(es, "eps_t", [128, 1], F32)

        def mk_ident():
            pool.memset(ident_f[:], 0.0)
            return pool.affine_select(out=ident_f[:], in_=ident_f[:], pattern=[[-1, 128]],
                                      compare_op=ALU.not_equal, fill=1.0, base=0, channel_multiplier=1)
        op("pool", mk_ident, writes=["ident_f"])
        op("dve", lambda: dve.tensor_copy(out=ident_b[:], in_=ident_f[:]), reads=["ident_f"], writes=["ident_b"])
        op("pool", lambda: pool.memset(ones_bf[:], 1.0), writes=["ones_bf"])
        op("pool", lambda: pool.memset(ones_row[:], 1.0), writes=["ones_row"])
        op("pool", lambda: pool.memset(eps_t[:], EPS), writes=["eps_t"])

        ct_sb = sbuf(es, "ct_sb", [128, 8], F32)
        op("sp", lambda: sp.dma_start(out=ct_sb[:], in_=ct_d[:, :]), writes=["ct_sb"], dma="ct")
        op("act", lambda: act.activation(out=cond[:], in_=ct_sb[:], func=AF.Silu), reads=["ct_sb"], writes=["cond"])

        def phase_params(i, stack):
            PH["gate_bc"] = sbuf(stack, "gate_bc%d" % i, [128, D], F32)
            PH["plg_bc"] = sbuf(stack, "plg_bc%d" % i, [128, D], F32)
            PH["plb_bc"] = sbuf(stack, "plb_bc%d" % i, [128, D], F32)
            gate_bc, plg_bc, plb_bc = PH["gate_bc"], PH["plg_bc"], PH["plb_bc"]
            op("sp", lambda: sp.dma_start(out=plg_bc[:], in_=plg_d[i:i + 1, :].partition_broadcast(128)),
               writes=["plg_bc"], dma="plg")
            op("sp", lambda: sp.dma_start(out=plb_bc[:], in_=plb_d[i:i + 1, :].partition_broadcast(128)),
               writes=["plb_bc"], dma="plb")
            with contextlib.ExitStack() as tmp:
                ones_f = sbuf(tmp, "ones_f%d" % i, [128, 128], F32)
                dg = sbuf(tmp, "dg%d" % i, [128, 8, 128], F32)
                op("pool", lambda: pool.memset(ones_f[:], 1.0), writes=["ones_f"])

                def mkd():
                    last = None
                    for j in range(8):
                        last = dve.tensor_scalar(out=dg[:, j, :], in0=ident_f[:], scalar1=modT[i][:, 16 + j:17 + j],
                                                 scalar2=None, op0=ALU.mult)
                    return last
                op("dve", mkd, reads=["ident_f", "modT%d" % i], writes=["dg"])

                def gbc():
                    last = None
                    for j in range(8):
                        last = pe.matmul(ps[:, 4 + j // 4, (j % 4) * 128:(j % 4 + 1) * 128], lhsT=ones_f[:], rhs=dg[:, j, :],
                                         start=True, stop=True)
                    return last
                op("pe", gbc, reads=["ones_f", "dg"], writes=B(4, 5))
                op("dve", lambda: dve.tensor_copy(out=gate_bc[:], in_=psflat[:, 4 * 512:6 * 512]),
                   reads=B(4, 5), writes=["gate_bc"])
                S.barrier()

        def mod_steps(i, stack, nslots, banks=(0, 1, 2), q="sp"):
            stg = [sbuf(stack, "adastg%d_%d" % (i, s), [128, 8, 128], F32) for s in range(nslots)]
            adab_sb = [sbuf(stack, "adab_sb%d_%d" % (i, s), [1, 128], F32) for s in range(nslots)]
            modrow = [sbuf(stack, "modrow%d_%d" % (i, s), [1, 128], F32) for s in range(2)]
            adav = adaw_d[i].rearrange("(kc p) n -> p kc n", p=128)

            qe = sp if q == "sp" else pool

            def load(j):
                s = j % nslots
                op(q, lambda: qe.dma_start(out=stg[s][:], in_=adav[:, :, j * 128:(j + 1) * 128]),
                   writes=["adastg%d" % s], dma="adastg%d" % s)
                op(q, lambda: qe.dma_start(out=adab_sb[s][:], in_=adab_d[i:i + 1, j * 128:(j + 1) * 128]),
                   writes=["adab_sb%d" % s], dma="adab%d" % s)

            def step(j):
                if j == 0:
                    for jj in range(min(nslots, 24)):
                        load(jj)
                s = j % nslots
                bk = banks[j % 2]
                ms = j % 2
                bc = banks[2]

                def mm():
                    last = None
                    for kc in range(8):
                        last = pe.matmul(ps[0:1, bk, 0:128], lhsT=cond[:, kc:kc + 1], rhs=stg[s][:, kc, :],
                                         start=(kc == 0), stop=(kc == 7))
                    return last
                op("pe", mm, reads=["cond", "adastg%d" % s], writes=B(bk))
                plus = 1.0 if 8 <= j < 16 else 0.0
                op("dve", lambda: dve.scalar_tensor_tensor(out=modrow[ms][:], in0=ps[0:1, bk, 0:128], scalar=plus,
                                                           in1=adab_sb[s][:], op0=ALU.add, op1=ALU.add),
                   reads=B(bk) + ["adab_sb%d" % s], writes=["modrow%d" % ms])
                op("pe", lambda: pe.matmul(ps[:, bc, 0:1], lhsT=modrow[ms][0:1, :], rhs=ones_row[0:1, 0:1],
                                           start=True, stop=True),
                   reads=["modrow%d" % ms, "ones_row"], writes=B(bc))
                op("dve", lambda: dve.tensor_copy(out=modT[i][:, j:j + 1], in_=ps[:, bc, 0:1]), reads=B(bc),
                   writes=["modT%d" % i])
                if j + nslots < 24:
                    load(j + nslots)

            def fin():
                pass
            return [(lambda j=j: step(j)) for j in range(24)], fin

        ST0 = {"stat": stat, "mv": mv, "rstd": rstd, "nmr": nmr, "sfx": ""}

        def epi_E1(yo_b0, xres_ap, xres_name, o_tile, o_name, T):
            sfx = T["sfx"]
            yo = psflat[:, yo_b0 * 512:(yo_b0 + 2) * 512]
            op("dve", lambda: dve.scalar_tensor_tensor(out=o_tile[:], in0=xres_ap, scalar=ALPHA, in1=yo,
                                                       op0=ALU.mult, op1=ALU.add),
               reads=B(yo_b0, yo_b0 + 1) + [xres_name], writes=[o_name])

            def st():
                dve.bn_stats(out=T["stat"][:, 0, :], in_=o_tile[:, 0:512])
                return dve.bn_stats(out=T["stat"][:, 1, :], in_=o_tile[:, 512:1024])
            op("dve", st, reads=[o_name], writes=["stat0" + sfx, "stat1" + sfx])
            op("dve", lambda: dve.bn_aggr(out=T["mv"][:], in_=T["stat"][:, 0:2, :].rearrange("p a b -> p (a b)")),
               reads=["stat0" + sfx, "stat1" + sfx], writes=["mv" + sfx])
            op("act", lambda: act.activation(out=T["rstd"][:], in_=T["mv"][:, 1:2], func=AF.Sqrt, bias=eps_t[:], scale=1.0),
               reads=["mv" + sfx, "eps_t"], writes=["rstd" + sfx])

        def epi_E2(o_tile, o_name, dst_ap, dst_key, T, gb="pool"):
            sfx = T["sfx"]
            op("dve", lambda: dve.reciprocal(out=T["rstd"][:], in_=T["rstd"][:]), reads=["rstd" + sfx], writes=["rstd" + sfx])
            op("dve", lambda: dve.scalar_tensor_tensor(out=T["nmr"][:], in0=T["mv"][:, 0:1], scalar=-1.0, in1=T["rstd"][:],
                                                       op0=ALU.mult, op1=ALU.mult),
               reads=["mv" + sfx, "rstd" + sfx], writes=["nmr" + sfx])
            op("act", lambda: act.activation(out=o_tile[:], in_=o_tile[:], func=AF.Identity, bias=T["nmr"][:],
                                             scale=T["rstd"][:]),
               reads=[o_name, "rstd" + sfx, "nmr" + sfx], writes=[o_name])
            g_eng = "pool" if gb == "pool" else "dve"
            ge = pool if g_eng == "pool" else dve
            op(g_eng, lambda: ge.tensor_tensor(out=o_tile[:], in0=o_tile[:], in1=PH["plg_bc"][:], op=ALU.mult),
               reads=[o_name, "plg_bc"], writes=[o_name])
            op("pool", lambda: pool.tensor_tensor(out=o_tile[:], in0=o_tile[:], in1=PH["plb_bc"][:], op=ALU.add),
               reads=[o_name, "plb_bc"], writes=[o_name])
            op("pool", lambda: pool.dma_start(out=dst_ap, in_=o_tile[:]), reads=[o_name], dma=dst_key)

        def epilogue(yo_b0, xres_ap, xres_name, o_tile, o_name, dst_ap, dst_key, gb="pool"):
            epi_E1(yo_b0, xres_ap, xres_name, o_tile, o_name, ST0)
            epi_E2(o_tile, o_name, dst_ap, dst_key, ST0, gb)

        with contextlib.ExitStack() as l0:
            hT = sbuf(l0, "hT", [128, 8, 16, TLOC // 16], BF16)
            base_reg = sbuf(l0, "base_reg", [128, 512], F32)
            base_first = sbuf(l0, "base_first", [128, 512], F32)

            with contextlib.ExitStack() as st0:
                tmpb = sbuf(st0, "tmpb", [128, 512], F32)
                op("pool", lambda: pool.iota(base_reg[:], pattern=[[0, 2], [128, 2], [-1, 128]], base=-64,
                                             channel_multiplier=1, allow_small_or_imprecise_dtypes=True),
                   writes=["base_reg"])
                op("dve", lambda: dve.scalar_tensor_tensor(out=base_reg[:], in0=base_reg[:], scalar=-1.0, in1=base_reg[:],
                                                           op0=ALU.mult, op1=ALU.min),
                   reads=["base_reg"], writes=["base_reg"])
                op("dve", lambda: dve.tensor_scalar(out=tmpb[:], in0=base_reg[:], scalar1=-64.0, scalar2=-NEGBIG,
                                                    op0=ALU.is_lt, op1=ALU.mult), reads=["base_reg"], writes=["tmpb"])
                op("dve", lambda: dve.tensor_tensor(out=base_reg[:], in0=base_reg[:], in1=tmpb[:], op=ALU.subtract),
                   reads=["base_reg", "tmpb"], writes=["base_reg"])
                op("dve", lambda: dve.tensor_copy(out=base_first[:], in_=base_reg[:]), reads=["base_reg"],
                   writes=["base_first"])
                op("dve", lambda: dve.memset(base_first[0:64, 0:128], NEGBIG), reads=["base_first"], writes=["base_first"])

                steps0, fin0 = mod_steps(0, st0, 6, q="pool")
                for st in steps0:
                    st()

                NXS = 8
                xs = [sbuf(st0, "xs%d" % s, [128, D], F32) for s in range(NXS)]
                for tt in range(TLOC // 128):
                    s = tt % NXS
                    op("sp", lambda: sp.dma_start(out=xs[s][:], in_=x_d[tt * 128:(tt + 1) * 128, :]),
                       writes=["xs%d" % s], dma="xs%d" % s)
                    b0 = 2 * (tt % 4)
                    pv = psflat[:, b0 * 512:(b0 + 2) * 512]

                    def tr():
                        last = None
                        for kc in range(8):
                            last = pe.transpose(pv[:, kc * 128:(kc + 1) * 128], xs[s][:, kc * 128:(kc + 1) * 128], ident_f[:])
                        return last
                    op("pe", tr, reads=["xs%d" % s, "ident_f"], writes=B(b0, b0 + 1))

                    def ev_act():
                        last = None
                        for kc in (0, 1, 2, 3):
                            last = act.activation(out=hT[:, kc, :, tt * 8:(tt + 1) * 8],
                                                  in_=pv[:, kc * 128:(kc + 1) * 128].rearrange("p (j r) -> p r j", r=16),
                                                  func=AF.Identity, bias=modT[0][:, kc:kc + 1], scale=modT[0][:, 8 + kc:9 + kc])
                        return last

                    def ev_dve():
                        last = None
                        for kc in (4, 5, 6, 7):
                            last = dve.tensor_scalar(out=hT[:, kc, :, tt * 8:(tt + 1) * 8],
                                                     in0=pv[:, kc * 128:(kc + 1) * 128].rearrange("p (j r) -> p r j", r=16),
                                                     scalar1=modT[0][:, 8 + kc:9 + kc], scalar2=modT[0][:, kc:kc + 1],
                                                     op0=ALU.mult, op1=ALU.add)
                        return last
                    op("act", ev_act, reads=B(b0) + ["modT0"], writes=["hTa%d" % tt])
                    op("dve", ev_dve, reads=B(b0 + 1) + ["modT0"], writes=["hTd%d" % tt])
                S.barrier()

            with contextlib.ExitStack() as ml:
                Wg = [sbuf(ml, "Wg%d" % s, [128, 8, 3, 128], BF16) for s in range(2)]
                Wgt = sbuf(ml, "Wgt", [128, 8, 128], BF16)
                QT = sbuf(ml, "QT", [128, TOWN], BF16)
                KT = sbuf(ml, "KT", [128, 6144], BF16)
                Vt = sbuf(ml, "Vt", [128, 48, 128], BF16)
                sg = sbuf(ml, "sg", [128, TOWN], BF16)
                acc = sbuf(ml, "acc", [128, 2, TOWN], F32)
                Ssb = [[sbuf(ml, "Ssb%d%d" % (a, s), [128, 512], F32) for s in range(2)] for a in range(2)]
                PT = [[sbuf(ml, "PT%d%d" % (a, s), [128, 512], BF16) for s in range(2)] for a in range(2)]
                yst = sbuf(ml, "yst", [128, TOWN], BF16)
                vst = [sbuf(ml, "vst%d" % s, [128, 512], BF16) for s in range(2)]
                vsl = [0]

                op("dve", lambda: dve.memset(KT[:], 0.0), writes=["KTall"])
                op("dve", lambda: dve.memset(Vt[:], 0.0), writes=["Vtall"])
                op("dve", lambda: dve.memset(vst[0][:], 0.0), writes=["vst0"])
                op("dve", lambda: dve.memset(vst[1][:], 0.0), writes=["vst1"])
                S.barrier()

                awv = awin_d.rearrange("(kc p) n -> p kc n", p=128)
                its = [(hp, g) for hp in range(8) for g in range(3)]

                def load_W(it):
                    hp, g = its[it]
                    s = it % 2

                    def f():
                        r = []
                        for wh in range(3):
                            c0 = g * 3072 + wh * 1024 + hp * 128
                            r.append(pool.dma_start(out=Wg[s][:, :, wh, :], in_=awv[:, :, c0:c0 + 128]))
                        return r
                    op("pool", f, writes=["Wg%d" % s], dma="Wg%d" % s)

                def load_Wgt(hp):
                    c0 = 9216 + hp * 128
                    op("pool", lambda: pool.dma_start(out=Wgt[:], in_=awv[:, :, c0:c0 + 128]), writes=["Wgt"], dma="Wgt")

                ipb = [0]

                def next_bank():
                    b = 6 + ipb[0] % 2
                    ipb[0] += 1
                    return b

                evq = [0]

                def evac_copy(out_ap, in_ap, reads, writes):
                    evq[0] += 1
                    if evq[0] % 3 != 0:
                        op("act", lambda: act.activation(out=out_ap, in_=in_ap, func=AF.Identity), reads=reads, writes=writes)
                    else:
                        op("dve", lambda: dve.tensor_copy(out=out_ap, in_=in_ap), reads=reads, writes=writes)

                norm_q = []
                norm_done = [8]

                def norm_upto(c):
                    while norm_q and norm_done[0] <= c:
                        norm_q.pop(0)()
                        norm_done[0] += 1

                load_W(0)
                load_Wgt(0)
                if not stop_after_l0:
                    steps1, fin1 = mod_steps(1, ml, 2, banks=(6, 7, 6))
                else:
                    steps1 = []

                for it, (hp, g) in enumerate(its):
                    if steps1:
                        steps1.pop(0)()
                    d = GROUP_D[g]
                    Lq = TOWN // d
                    Lk = Lq + 64
                    Lkp = Lk + 64
                    ntr = Lkp // 128
                    ws = it % 2
                    W = Wg[ws]
                    wname = "Wg%d" % ws
                    if it + 1 < len(its):
                        load_W(it + 1)

                    km = 16 // d

                    def hrhs(kc, r, j0, n):
                        v = hT[:, kc, :, :].rearrange("p (m r) q -> p r m q", r=d)
                        return v[:, r, :, j0 // km:(j0 + n) // km]

                    def psmq(bk, o, n):
                        return ps[:, bk, o:o + n].rearrange("p (m q) -> p m q", m=km)

                    def pqm(bk, o, n):
                        return ps[:, bk, o:o + n].rearrange("p (m q) -> p q m", m=km)

                    def unperm(ap2d):
                        return ap2d.rearrange("p (q m) -> p q m", m=km)

                    def proj(wh, bk, r, j0, n, o=0):
                        def mm():
                            last = None
                            for kc in range(8):
                                last = pe.matmul(psmq(bk, o, n), lhsT=W[:, kc, wh, :], rhs=hrhs(kc, r, j0, n),
                                                 start=(kc == 0), stop=(kc == 7))
                            return last
                        op("pe", mm, reads=[wname], writes=B(bk))

                    def q_stage(r, i):
                        if Lq >= 512:
                            if i * 512 >= Lq:
                                return
                            bk = next_bank()
                            pi0 = r * Lq + i * 512
                            proj(0, bk, r, i * 512, 512)
                            evac_copy(unperm(QT[:, pi0:pi0 + 512]), pqm(bk, 0, 512), B(bk), ["QT%d" % (pi0 // 512)])
                        else:
                            if i != 0 or r % 2 != 0:
                                return
                            bk = next_bank()
                            pi0 = r * Lq

                            def mm():
                                last = None
                                for kc in range(8):
                                    last = pe.matmul(ps[:, bk, :].rearrange("p (a b) -> p a b", a=2), lhsT=W[:, kc, 0, :],
                                                     rhs=hT[:, kc, r:r + 2, 0:Lq], start=(kc == 0), stop=(kc == 7))
                                return last
                            op("pe", mm, reads=[wname], writes=B(bk))
                            evac_copy(QT[:, pi0:pi0 + 512], ps[:, bk, :], B(bk), ["QT%d" % (pi0 // 512)])

                    def k_stage(r, i):
                        j0 = i * 512
                        if j0 >= Lk:
                            return
                        n = min(512, Lk - j0)
                        bk = next_bank()
                        proj(1, bk, r, j0, n)
                        o0 = r * Lkp + 64 + j0
                        evac_copy(unperm(KT[:, o0:o0 + n]), pqm(bk, 0, n), B(bk), ["KT_%d_%d" % (r, i)])

                    def v_stage_mm(r, i):
                        c0 = 4 * i
                        if c0 >= ntr:
                            return None
                        nt = min(4, ntr - c0)
                        off = 64 if c0 == 0 else 0
                        jst = 128 * c0 - 64 + off
                        n = 128 * nt - off
                        bk = next_bank()
                        vs = vsl[0] % 2
                        vsl[0] += 1
                        proj(2, bk, r, jst, n, o=off)
                        evac_copy(unperm(vst[vs][:, off:off + n]), pqm(bk, off, n), B(bk), ["vst%d" % vs])
                        return (c0, nt, vs)

                    def v_stage_tr(r, st):
                        if st is None:
                            return
                        c0, nt, vs = st
                        bk2 = next_bank()
                        ptb = ps[:, bk2, :].bitcast(BF16)

                        def tr():
                            last = None
                            for ti in range(nt):
                                last = pe.transpose(ptb[:, ti * 128:(ti + 1) * 128], vst[vs][:, ti * 128:(ti + 1) * 128], ident_b[:])
                            return last
                        op("pe", tr, reads=["vst%d" % vs, "ident_b"], writes=B(bk2))
                        t0 = r * ntr + c0
                        evac_copy(Vt[:, t0:t0 + nt, :], ptb[:, 0:nt * 128].rearrange("p (a b) -> p a b", a=nt), B(bk2),
                                  ["V%d" % (t0 + ti) for ti in range(nt)])

                    def gate_stage(c):
                        norm_upto(c)
                        bk = next_bank()

                        def mm():
                            last = None
                            for kc in range(8):
                                last = pe.matmul(ps[:, bk, :].rearrange("p (m q) -> p m q", m=16), lhsT=Wgt[:, kc, :],
                                                 rhs=hT[:, kc, :, c * 32:(c + 1) * 32], start=(kc == 0), stop=(kc == 7))
                            return last
                        op("pe", mm, reads=["Wgt"], writes=B(bk))
                        op("act", lambda: act.activation(out=sg[:, c * 512:(c + 1) * 512].rearrange("p (q m) -> p q m", m=16),
                                                         in_=ps[:, bk, :].rearrange("p (m q) -> p q m", m=16), func=AF.Silu),
                           reads=B(bk), writes=["sg%d" % c])

                    cs = [8.0 * SLOPES[g][2 * hp + a] * d for a in range(2)]

                    def kt_chunks(r, pb):
                        lo = max(0, 256 * pb - 64) // 512
                        hi = min(Lk - 1, 256 * pb + 319) // 512
                        return ["KT_%d_%d" % (r, i) for i in range(lo, hi + 1)]

                    def emit_qk(blk):
                        r, pb, bi = blk
                        sl = bi % 2
                        qname = "QT%d" % ((r * Lq + 256 * pb) // 512)
                        for a in range(2):
                            bk = 2 * sl + a

                            def mm():
                                last = None
                                for qb2 in range(2):
                                    b = 2 * pb + qb2
                                    for kt in range(2):
                                        c = b + kt
                                        col = (qb2 * 2 + kt) * 128
                                        last = pe.matmul(ps[:, bk, col:col + 128],
                                                         lhsT=KT[64 * a:64 * a + 64, r * Lkp + 128 * c: r * Lkp + 128 * c + 128],
                                                         rhs=QT[64 * a:64 * a + 64, r * Lq + 128 * b: r * Lq + 128 * b + 128],
                                                         start=True, stop=True)
                                return last
                            op("pe", mm, reads=[qname] + kt_chunks(r, pb), writes=B(bk))

                    def emit_sm(blk):
                        r, pb, bi = blk
                        sl = bi % 2
                        bt = base_first if pb == 0 else base_reg
                        for a in range(2):
                            bk = 2 * sl + a
                            op("dve", lambda: dve.scalar_tensor_tensor(out=Ssb[a][sl][:], in0=bt[:], scalar=cs[a], in1=ps[:, bk, :],
                                                                       op0=ALU.mult, op1=ALU.add),
                               reads=B(bk), writes=["Ssb%d%d" % (a, sl)])
                            op("act", lambda: act.activation(out=PT[a][sl][:], in_=Ssb[a][sl][:], func=AF.Exp, scale=0.125),
                               reads=["Ssb%d%d" % (a, sl)], writes=["PT%d%d" % (a, sl)])

                    def emit_pv(blk):
                        r, pb, bi = blk
                        sl = bi % 2
                        bk = 4 + (bi % 2)
                        vnames = ["V%d" % (r * ntr + 2 * pb + k) for k in range(3)]

                        for a in range(2):
                            def mm():
                                last = None
                                for qb2 in range(2):
                                    b = 2 * pb + qb2
                                    for nd in range(2):
                                        for kt in range(2):
                                            tau = r * ntr + b + kt
                                            col = (qb2 * 2 + kt) * 128
                                            lh = Vt[:, tau, 64 * a:64 * a + 64] if nd == 0 else ones_bf[:, 0:64]
                                            oc = nd * 256 + qb2 * 128
                                            last = pe.matmul(ps[64 * a:64 * a + 64, bk, oc:oc + 128], lhsT=lh,
                                                             rhs=PT[a][sl][:, col:col + 128], start=(kt == 0), stop=(kt == 1),
                                                             tile_position=(0, 64 * a))
                                return last
                            op("pe", mm, reads=["PT%d%d" % (a, sl), "ones_bf"] + vnames, writes=B(bk))
                        accv = acc[:].rearrange("p n (j r) -> p n r j", r=d)[:, :, r, 256 * pb:256 * pb + 256]
                        odv = ps[:, bk, :].rearrange("p (n q) -> p n q", n=2)
                        if g == 0:
                            op("dve", lambda: dve.tensor_copy(out=accv, in_=odv), reads=B(bk), writes=["acc"])
                        else:
                            op("dve", lambda: dve.tensor_tensor(out=accv, in0=accv, in1=odv, op=ALU.add),
                               reads=B(bk) + ["acc"], writes=["acc"])

                    nst = max((Lk + 511) // 512, (ntr + 3) // 4)
                    G = []
                    for r in range(d):
                        for i in range(nst):
                            vbox = [None]

                            def g_vmm(r=r, i=i, vbox=vbox):
                                vbox[0] = v_stage_mm(r, i)

                            def g_k(r=r, i=i):
                                k_stage(r, i)

                            def g_vtr(r=r, vbox=vbox):
                                v_stage_tr(r, vbox[0])

                            def g_q(r=r, i=i):
                                q_stage(r, i)
                            for fn in (g_vmm, g_k, g_vtr, g_q):
                                G.append(((r, i), fn))
                    gates = list(range(TOWN // 512)) if g == 0 else []
                    gi = [0]

                    def emit_group():
                        key, fn = G[gi[0]]
                        gi[0] += 1
                        fn()
                        if gates and gi[0] % max(1, len(G) // 8) == 0:
                            gate_stage(gates.pop(0))

                    pending = None
                    bi = 0
                    for r in range(d):
                        for pb in range(Lq // 256):
                            req = (r, (256 * pb + 383) // 512)
                            while gi[0] < len(G) and G[gi[0]][0] <= req:
                                emit_group()
                            blk = (r, pb, bi)
                            bi += 1
                            if g == 0:
                                norm_upto(pb // 2)
                            else:
                                norm_upto(7)
                            emit_qk(blk)
                            emit_sm(blk)
                            if gi[0] < len(G):
                                emit_group()
                            if pending is not None:
                                emit_pv(pending)
                            pending = blk
                    while gi[0] < len(G):
                        emit_group()
                    if pending is not None:
                        emit_pv(pending)
                    if g == 0:
                        while gates:
                            gate_stage(gates.pop(0))
                        if hp + 1 < 8:
                            load_Wgt(hp + 1)

                    if g == 2:
                        def mk_piece(c, hp=hp):
                            def piece():
                                sl_ = slice(c * 512, (c + 1) * 512)
                                op("dve", lambda: dve.reciprocal(out=acc[:, 1, sl_], in_=acc[:, 1, sl_]), reads=["acc"], writes=["acc"])
                                op("dve", lambda: dve.tensor_tensor(out=acc[:, 0, sl_], in0=acc[:, 0, sl_], in1=acc[:, 1, sl_],
                                                                    op=ALU.mult), reads=["acc"], writes=["acc"])
                                op("dve", lambda: dve.tensor_tensor(out=yst[:, sl_], in0=acc[:, 0, sl_], in1=sg[:, sl_], op=ALU.mult),
                                   reads=["acc", "sg%d" % c], writes=["yst"])
                                if c == 7:
                                    op("sp", lambda: sp.dma_start(out=yts_d[:, hp, :], in_=yst[:]), reads=["yst"], dma="yst")
                            return piece
                        norm_q.extend(mk_piece(c) for c in range(8))
                        norm_done[0] = 0
                norm_upto(7)
                S.barrier()
        S.barrier()

        if not stop_after_l0:
            Wib = sbuf(es, "Wib", [128, 8, 6144], BF16)
            Wob = sbuf(es, "Wob", [128, 16, D], BF16)
            wsT = sbuf(es, "wsT", [128, 16, 128], BF16)
        with contextlib.ExitStack() as lo:
            WoA = sbuf(lo, "WoA", [128, 8, D], BF16)
            op("pool", lambda: pool.dma_start(out=WoA[:], in_=awout_d.rearrange("(kc p) n -> p kc n", p=128)),
               writes=["WoA"], dma="WoA")
            phase_params(0, lo)
            for kc in range(8):
                op("dve" if kc % 2 == 0 else "pool",
                   (lambda kc=kc: (dve if kc % 2 == 0 else pool).tensor_tensor(out=WoA[:, kc, :], in0=WoA[:, kc, :],
                                                                                in1=PH["gate_bc"][:], op=ALU.mult)),
                   reads=["WoA", "gate_bc"], writes=["WoA_g%d" % kc])
            S.barrier()
            if not stop_after_l0:
                bwv = bwin_d.rearrange("(kc p) n -> p kc n", p=128)
                for kc in range(8):
                    op("pool", lambda: pool.dma_start(out=Wib[:, kc, :], in_=bwv[:, kc, :]), writes=["Wib%d" % kc], dma="Wib%d" % kc)
                bov = bwout_d.rearrange("(kc p) n -> p kc n", p=128)
                for q in range(4):
                    op("pool", lambda: pool.dma_start(out=Wob[:, 4 * q:4 * q + 4, :], in_=bov[:, 4 * q:4 * q + 4, :]),
                       writes=["Wob%d" % q], dma="Wob%d" % q)
                op("pool", lambda: pool.dma_start(out=wsT[:], in_=wsT_d[:, :, :]), writes=["wsT"], dma="wsT")
            yin = [sbuf(lo, "yin%d" % s, [128, 8, 512], BF16) for s in range(2)]
            xr = [sbuf(lo, "xr%d" % s, [128, D], F32) for s in range(2)]
            ot = [sbuf(lo, "ot%d" % s, [128, D], F32) for s in range(4)]
            def lo_loads(tt):
                q4, ys = tt // 4, (tt // 4) % 2
                if tt % 4 == 0:
                    op("sp", lambda: sp.dma_start(out=yin[ys][:], in_=yts_d[:, :, q4 * 512:(q4 + 1) * 512]),
                       writes=["yin%d" % ys], dma="yin%d" % ys)
                s = tt % 2
                op("sp", lambda: sp.dma_start(out=xr[s][:], in_=x_d[tt * 128:(tt + 1) * 128, :]),
                   writes=["xr%d" % s], dma="xr%d" % s)
            STs = []
            for q in range(2):
                STs.append({"stat": sbuf(lo, "lstat%d" % q, [128, 2, 6], F32), "mv": sbuf(lo, "lmv%d" % q, [128, 2], F32),
                            "rstd": sbuf(lo, "lrstd%d" % q, [128, 1], F32), "nmr": sbuf(lo, "lnmr%d" % q, [128, 1], F32),
                            "sfx": "_l%d" % q})
            NT = TOWN // 128

            def lo_E1(tt):
                q4, ys = tt // 4, (tt // 4) % 2
                s = tt % 2
                b0 = 2 * (tt % 4)

                def mm():
                    last = None
                    for hf in range(2):
                        for kc in range(8):
                            last = pe.matmul(ps[:, b0 + hf, :], lhsT=yin[ys][:, kc, (tt % 4) * 128:(tt % 4 + 1) * 128],
                                             rhs=WoA[:, kc, hf * 512:(hf + 1) * 512], start=(kc == 0), stop=(kc == 7))
                    return last
                op("pe", mm, reads=["yin%d" % ys, "WoA"], writes=B(b0, b0 + 1))
                epi_E1(b0, xr[s][:], "xr%d" % s, ot[tt % 4], "ot%d" % (tt % 4), STs[s])

            def lo_E2(tt):
                s = tt % 2
                epi_E2(ot[tt % 4], "ot%d" % (tt % 4), x1s_d[tt * 128:(tt + 1) * 128, :], "ot%d" % (tt % 4), STs[s], gb="dve")

            lo_loads(0)
            lo_loads(1)
            lo_E1(0)
            for tt in range(NT):
                if tt + 1 < NT:
                    lo_E1(tt + 1)
                lo_E2(tt)
                if tt + 2 < NT:
                    lo_loads(tt + 2)
            S.barrier()
        S.barrier()

        if not stop_after_l0:
            with contextlib.ExitStack() as l1:
                phase_params(1, l1)
                for kc in range(16):
                    op("dve" if kc % 2 == 0 else "pool",
                       (lambda kc=kc: (dve if kc % 2 == 0 else pool).tensor_tensor(out=Wob[:, kc, :], in0=Wob[:, kc, :],
                                                                                    in1=PH["gate_bc"][:], op=ALU.mult)),
                       reads=["gate_bc"], writes=["Wob_g%d" % kc])
                Cb = sbuf(l1, "Cb", [128, 16, 128], F32)
                lng_bc = sbuf(l1, "lng_bc", [128, 2048], F32)
                op("sp", lambda: sp.dma_start(out=lng_bc[:], in_=blg_d[0:1, :].partition_broadcast(128)),
                   writes=["lng_bc"], dma="lng")
                with contextlib.ExitStack() as s1:
                    lnb_t = sbuf(s1, "lnb_t", [128, 16, 128], F32)
                    wsN = sbuf(s1, "wsN", [128, 16, 128], F32)
                    bsT = sbuf(s1, "bsT", [128, 16], F32)
                    rsw = sbuf(s1, "rsw", [128, 16], F32)
                    op("sp", lambda: sp.dma_start(out=lnb_t[:].rearrange("p a b -> p (a b)"),
                                                  in_=blb_d[0:1, :].partition_broadcast(128)), writes=["lnb_t"], dma="lnb")
                    op("sp", lambda: sp.dma_start(out=wsN[:], in_=wsN_d[:, :, :]), writes=["wsN"], dma="wsN")
                    op("sp", lambda: sp.dma_start(out=bsT[:], in_=bsT_d[:, :]), writes=["bsT"], dma="bsT")
                    op("dve", lambda: dve.reduce_sum(out=rsw[:], in_=wsN[:], axis=mybir.AxisListType.X),
                       reads=["wsN"], writes=["rsw"])
                    op("dve", lambda: dve.tensor_tensor(out=Cb[:], in0=lnb_t[:], in1=rsw[:].unsqueeze(2).to_broadcast([128, 16, 128]),
                                                        op=ALU.mult), reads=["lnb_t", "rsw"], writes=["Cb"])
                    op("dve", lambda: dve.tensor_tensor(out=Cb[:], in0=Cb[:], in1=bsT[:].unsqueeze(2).to_broadcast([128, 16, 128]),
                                                        op=ALU.add), reads=["Cb", "bsT"], writes=["Cb"])
                    S.barrier()
                Cflat = Cb[:].rearrange("p a b -> p (a b)")

                x1t = [sbuf(l1, "x1t%d" % s, [128, D], F32) for s in range(2)]
                h1Ts = [sbuf(l1, "h1T%d" % q, [128, 8, 128], BF16) for q in range(2)]
                vf = sbuf(l1, "vf", [128, 2048], F32)
                vhat = sbuf(l1, "vhat", [128, 2048], BF16)
                ubf = sbuf(l1, "ubf", [128, 2048], BF16)
                sgb = sbuf(l1, "sgb", [128, 2048], BF16)
                yT = sbuf(l1, "yT", [128, 16, 128], BF16)
                o1s = [sbuf(l1, "o1_%d" % q, [128, D], F32) for q in range(2)]
                wib_all = ["Wib%d" % kc for kc in range(8)]
                wob_all = ["Wob%d" % q for q in range(4)]
                NCH = TOWN // 128

                stat2 = sbuf(l1, "stat2", [128, 4, 6], F32)
                mv2 = sbuf(l1, "mv2", [128, 2], F32)
                rstd2 = sbuf(l1, "rstd2", [128, 1], F32)
                nmr2 = sbuf(l1, "nmr2", [128, 1], F32)
                vfn = ["vf%d" % j for j in range(4)]
                ubn = ["ubf%d" % j for j in range(4)]
                sgn1 = ["sgb%d" % j for j in range(4)]
                rot = [0]

                def load_x1(tt):
                    s = tt % 2
                    op("sp", lambda: sp.dma_start(out=x1t[s][:], in_=x1s_d[tt * 128:(tt + 1) * 128, :]),
                       writes=["x1t%d" % s], dma="x1t%d" % s)

                def stage_A(tt):
                    s = tt % 2
                    h1T = h1Ts[s]
                    pv = psflat[:, 0:1024]

                    def tr():
                        last = None
                        for kc in range(8):
                            last = pe.transpose(pv[:, kc * 128:(kc + 1) * 128], x1t[s][:, kc * 128:(kc + 1) * 128], ident_f[:])
                        return last
                    op("pe", tr, reads=["x1t%d" % s, "ident_f"], writes=B(0, 1))

                    def ev_act():
                        last = None
                        for kc in range(8):
                            last = act.activation(out=h1T[:, kc, :], in_=pv[:, kc * 128:(kc + 1) * 128], func=AF.Identity,
                                                  bias=modT[1][:, kc:kc + 1], scale=modT[1][:, 8 + kc:9 + kc])
                        return last
                    op("act", ev_act, reads=B(0, 1) + ["modT1"], writes=["h1T%d" % s])

                def inproj_chunk(tt, cc):
                    bk = rot[0] % 4
                    rot[0] += 1
                    h1T = h1Ts[tt % 2]

                    def mm():
                        last = None
                        for kc in range(8):
                            last = pe.matmul(ps[:, bk, :], lhsT=h1T[:, kc, :], rhs=Wib[:, kc, cc * 512:(cc + 1) * 512],
                                             start=(kc == 0), stop=(kc == 7))
                        return last
                    op("pe", mm, reads=["h1T%d" % (tt % 2)] + wib_all, writes=B(bk))
                    return bk

                def stage_B1a(tt):
                    for j in range(4):
                        bk = inproj_chunk(tt, 4 + j)
                        op("act", lambda: act.activation(out=vf[:, j * 512:(j + 1) * 512], in_=ps[:, bk, :], func=AF.Gelu),
                           reads=B(bk), writes=["vf%d" % j])

                def stage_B1b(tt):
                    for j in range(4):
                        op("dve", lambda: dve.bn_stats(out=stat2[:, j, :], in_=vf[:, j * 512:(j + 1) * 512]),
                           reads=["vf%d" % j], writes=["stat2_%d" % j])
                    op("dve", lambda: dve.bn_aggr(out=mv2[:], in_=stat2[:].rearrange("p a b -> p (a b)")),
                       reads=["stat2_%d" % j for j in range(4)], writes=["mv2"])
                    op("act", lambda: act.activation(out=rstd2[:], in_=mv2[:, 1:2], func=AF.Sqrt, bias=eps_t[:], scale=1.0),
                       reads=["mv2", "eps_t"], writes=["rstd2"])
                    op("dve", lambda: dve.reciprocal(out=rstd2[:], in_=rstd2[:]), reads=["rstd2"], writes=["rstd2"])
                    op("dve", lambda: dve.scalar_tensor_tensor(out=nmr2[:], in0=mv2[:, 0:1], scalar=-1.0, in1=rstd2[:],
                                                               op0=ALU.mult, op1=ALU.mult),
                       reads=["mv2", "rstd2"], writes=["nmr2"])
                    op("act", lambda: act.activation(out=vhat[:], in_=vf[:], func=AF.Identity, bias=nmr2[:], scale=rstd2[:]),
                       reads=vfn + ["rstd2", "nmr2"], writes=["vhat"])

                def stage_B2(tt, js):
                    for j8 in js:
                        if j8 < 4:
                            j = j8
                            bk = inproj_chunk(tt, 8 + j)
                            op("act", lambda: act.activation(out=sgb[:, j * 512:(j + 1) * 512], in_=ps[:, bk, :], func=AF.Silu),
                               reads=B(bk), writes=["sgb%d" % j])
                        else:
                            j = j8 - 4
                            bk = inproj_chunk(tt, j)
                            op("act", lambda: act.activation(out=ubf[:, j * 512:(j + 1) * 512], in_=ps[:, bk, :], func=AF.Gelu),
                               reads=B(bk), writes=["ubf%d" % j])

                def stage_C(tt):
                    def sp_mm():
                        last = None
                        for gi in range(16):
                            last = pe.matmul(ps[:, 4 + gi // 4, (gi % 4) * 128:(gi % 4 + 1) * 128], lhsT=wsT[:, gi, :],
                                             rhs=vhat[:, gi * 128:(gi + 1) * 128], start=True, stop=True)
                        return last
                    op("pe", sp_mm, reads=["vhat", "wsT"], writes=B(4, 5, 6, 7))
                    svp = psflat[:, 4 * 512:8 * 512]
                    op("dve", lambda: dve.tensor_tensor(out=vf[:], in0=svp, in1=lng_bc[:], op=ALU.mult),
                       reads=B(4, 5, 6, 7) + ["lng_bc"] + vfn, writes=vfn)
                    op("dve", lambda: dve.tensor_tensor(out=vf[:], in0=vf[:], in1=Cflat, op=ALU.add), reads=vfn + ["Cb"], writes=vfn)
                    op("dve", lambda: dve.tensor_tensor(out=vf[:], in0=vf[:], in1=ubf[:], op=ALU.mult),
                       reads=vfn + ubn, writes=vfn)
                    op("dve", lambda: dve.tensor_tensor(out=ubf[:], in0=vf[:], in1=sgb[:], op=ALU.mult),
                       reads=vfn + sgn1 + ubn, writes=ubn)

                def stage_D(tt):
                    ytp = psflat[:, 4 * 512:6 * 512].bitcast(BF16)

                    def ytr():
                        last = None
                        for gi in range(16):
                            last = pe.transpose(ytp[:, gi * 128:(gi + 1) * 128], ubf[:, gi * 128:(gi + 1) * 128], ident_b[:])
                        return last
                    op("pe", ytr, reads=ubn + ["ident_b"], writes=B(4, 5))
                    op("dve", lambda: dve.tensor_copy(out=yT[:].rearrange("p a b -> p (a b)"), in_=ytp),
                       reads=B(4, 5), writes=["yT"])

                def stage_E(tt):
                    s = tt % 2

                    def omm():
                        last = None
                        for hf in range(2):
                            for kc in range(16):
                                last = pe.matmul(ps[:, 6 + hf, :], lhsT=yT[:, kc, :], rhs=Wob[:, kc, hf * 512:(hf + 1) * 512],
                                                 start=(kc == 0), stop=(kc == 15))
                        return last
                    op("pe", omm, reads=["yT"] + wob_all, writes=B(6, 7))
                    epilogue(6, x1t[s][:], "x1t%d" % s, o1s[s], "o1_%d" % s, out_d[tt * 128:(tt + 1) * 128, :], "o1_%d" % s)

                load_x1(0)
                load_x1(1)
                stage_A(0)
                stage_B1a(0)
                stage_B1b(0)
                stage_B2(0, range(8))
                stage_A(1)
                for tt in range(NCH):
                    nx = tt + 1 < NCH
                    stage_C(tt)
                    if nx:
                        stage_B1a(tt + 1)
                    stage_D(tt)
                    if nx:
                        stage_B1b(tt + 1)
                        stage_B2(tt + 1, range(0, 2))
                    stage_E(tt)
                    if tt + 2 < NCH:
                        load_x1(tt + 2)
                    if nx:
                        stage_B2(tt + 1, range(2, 8))
                    if tt + 2 < NCH:
                        stage_A(tt + 2)
                S.barrier()
        S.barrier()
    return nc


def _prep_inputs(inputs):
    x = np.asarray(inputs["x"], dtype=np.float32)
    c = np.asarray(inputs["c"], dtype=np.float32)
    w_s = np.asarray(inputs["b_w_s"], dtype=np.float32)[0]
    b_s = np.asarray(inputs["b_b_s"], dtype=np.float32)[0]
    shared = {
        "ada_w": np.ascontiguousarray(inputs["ada_w"], dtype=np.float32),
        "ada_b": np.ascontiguousarray(inputs["ada_b"], dtype=np.float32),
        "post_ln_g": np.ascontiguousarray(inputs["post_ln_g"], dtype=np.float32),
        "post_ln_b": np.ascontiguousarray(inputs["post_ln_b"], dtype=np.float32),
        "a_w_in": np.ascontiguousarray(np.asarray(inputs["a_w_in"], dtype=np.float32)[0]),
        "a_w_out": np.ascontiguousarray(np.asarray(inputs["a_w_out"], dtype=np.float32)[0]),
        "b_w_in": np.ascontiguousarray(np.asarray(inputs["b_w_in"], dtype=np.float32)[0]),
        "b_ln_g": np.ascontiguousarray(np.asarray(inputs["b_ln_g"], dtype=np.float32)[0:1]),
        "b_ln_b": np.ascontiguousarray(np.asarray(inputs["b_ln_b"], dtype=np.float32)[0:1]),
        "b_w_out": np.ascontiguousarray(np.asarray(inputs["b_w_out"], dtype=np.float32)[0]),
    }
    in_maps = []
    for core in range(8):
        b, half = core // 2, core % 2
        if half == 0:
            xl = x[b, 0:TLOC]
            ws, bs = w_s, b_s
        else:
            xl = x[b, ::-1][0:TLOC]
            ws, bs = w_s[:, ::-1, ::-1], b_s[:, ::-1]
        m = dict(shared)
        m["x"] = np.ascontiguousarray(xl)
        m["ct"] = np.ascontiguousarray(c[b].reshape(8, 128).T)
        m["w_sT"] = np.ascontiguousarray(ws.transpose(2, 0, 1))
        m["w_sN"] = np.ascontiguousarray(ws.transpose(1, 0, 2))
        m["b_sT"] = np.ascontiguousarray(bs.T)
        in_maps.append(m)
    return in_maps


def kernel(**inputs):
    in_maps = _prep_inputs(inputs)
    nc = build()
    res = run_bass_kernel_spmd(nc, in_maps, core_ids=list(range(8)))
    out = np.empty((4, SEQ, D), dtype=np.float32)
    for core in range(8):
        b, half = core // 2, core % 2
        o = np.asarray(res.results[core]["out"], dtype=np.float32)
        if half == 0:
            out[b, 0:TOWN] = o
        else:
            out[b, TOWN:SEQ] = o[::-1]
    return out
```
